# Optimizing a Trainium2 kernel written in Bass

```python
import math
import jax, jax.numpy as jnp
from jax import lax
import numpy as np

D_MODEL = 1024
BATCH = 2
SEQ = 8192
DEPTH = 4
DEC_BATCH = 8
DEC_SEQ = 8192
PAST_LEN = 128

N_HEADS = 8
HEAD_DIM = 128
N_KV_HEADS = 2
GROUP = N_HEADS // N_KV_HEADS
WINDOW = 128
BLK = 128
Q_W = N_HEADS * HEAD_DIM
KV_W = N_KV_HEADS * HEAD_DIM
NUM_BUCKETS = 32
MAX_DISTANCE = 128
D_RNN = 1024
RNN_BLOCKS = 8
RNN_BW = D_RNN // RNN_BLOCKS
CONV_W = 4
CONV_LEFT = 2
LRU_C = 8.0
D_FF = 2816
IN_W = Q_W + 2 * KV_W + 2 * D_RNN + 2 * D_MODEL
SPLIT_POINTS = [Q_W, Q_W + KV_W, Q_W + 2 * KV_W, Q_W + 2 * KV_W + D_RNN,
                Q_W + 2 * KV_W + 2 * D_RNN, Q_W + 2 * KV_W + 2 * D_RNN + D_MODEL]
EPS = 1e-6
NEG_INF = -1e30

kernel_name = "hybrid_bidir_local_gqa_rglru_macaron"


def rms_norm(x, g):
    xf = x.astype(jnp.float32)
    y = xf * lax.rsqrt(jnp.mean(xf * xf, axis=-1, keepdims=True) + EPS)
    return (y * g.astype(jnp.float32)).astype(x.dtype)


def swiglu_ffn(x, w_up, w_down):
    gate, up = jnp.split(x @ w_up, 2, axis=-1)
    return (jax.nn.silu(gate) * up) @ w_down


def t5_buckets(rel):
    n = NUM_BUCKETS // 2
    max_exact = n // 2
    ret = (rel > 0).astype(np.int32) * n
    na = np.abs(rel)
    large = max_exact + (np.log(np.maximum(na, 1) / max_exact)
                         / math.log(MAX_DISTANCE / max_exact) * (n - max_exact)).astype(np.int32)
    large = np.minimum(large, n - 1)
    return ret + np.where(na < max_exact, na, large)


def windowed_attention(q, k, v, sink, rel_table):
    B, S = q.shape[0], q.shape[1]
    nb = S // BLK
    qb = q.reshape(B, nb, BLK, N_KV_HEADS, GROUP, HEAD_DIM)

    def band(t):
        tp = jnp.pad(t, ((0, 0), (BLK, BLK), (0, 0), (0, 0))).reshape(B, nb + 2, BLK, N_KV_HEADS, HEAD_DIM)
        return jnp.concatenate([tp[:, :-2], tp[:, 1:-1], tp[:, 2:]], axis=2)

    kb, vb = band(k), band(v)
    scores = jnp.einsum('bnqkgd,bnskd->bnkgqs', qb, kb).astype(jnp.float32) * (HEAD_DIM ** -0.5)

    rel = np.arange(3 * BLK)[None, :] - BLK - np.arange(BLK)[:, None]
    bias = rel_table.astype(jnp.float32)[t5_buckets(rel)]
    bias = jnp.transpose(bias, (2, 0, 1)).reshape(N_KV_HEADS, GROUP, BLK, 3 * BLK)
    jpos = np.arange(nb)[:, None] * BLK - BLK + np.arange(3 * BLK)[None, :]
    mask = (np.abs(rel) <= WINDOW)[None] & ((jpos >= 0) & (jpos < S))[:, None, :]
    mask = mask[None, :, None, None]

    scores = jnp.where(mask, scores + bias, NEG_INF)
    s_h = sink.astype(jnp.float32).reshape(N_KV_HEADS, GROUP, 1, 1)
    m = jnp.maximum(jnp.max(scores, axis=-1, keepdims=True), s_h)
    p = jnp.exp(scores - m)
    probs = p / (jnp.sum(p, axis=-1, keepdims=True) + jnp.exp(s_h - m))
    out = jnp.einsum('bnkgqs,bnskd->bnqkgd', probs.astype(v.dtype), vb)
    return out.reshape(B, S, Q_W)


def depthwise_conv(x, w, b):
    S = x.shape[1]
    xp = jnp.pad(x, ((0, 0), (CONV_LEFT, CONV_W - 1 - CONV_LEFT), (0, 0)))
    y = b
    for tap in range(CONV_W):
        y = y + xp[:, tap:tap + S] * w[tap]
    return y


def linear_scan(a, b):
    def combine(left, right):
        a1, b1 = left
        a2, b2 = right
        return a1 * a2, a2 * b1 + b2
    _, h = lax.associative_scan(combine, (a, b), axis=1)
    return h


def rg_lru_direction(x, lam, w_a, b_a, w_x, b_x, reverse):
    B, S, _ = x.shape
    xb = x.reshape(B, S, RNN_BLOCKS, RNN_BW)
    r = jax.nn.sigmoid(jnp.einsum('bsnc,ncd->bsnd', xb, w_a.astype(jnp.float32)).reshape(B, S, D_RNN)
                       + b_a.astype(jnp.float32))
    i = jax.nn.sigmoid(jnp.einsum('bsnc,ncd->bsnd', xb, w_x.astype(jnp.float32)).reshape(B, S, D_RNN)
                       + b_x.astype(jnp.float32))
    log_a = -LRU_C * r * jax.nn.softplus(-lam.astype(jnp.float32))
    a = jnp.exp(log_a)
    b = jnp.sqrt(-jnp.expm1(2.0 * log_a)) * (i * x)
    if reverse:
        return linear_scan(a[:, ::-1], b[:, ::-1])[:, ::-1]
    return linear_scan(a, b)


def mixer(h, w_in, conv_w, conv_b, lam, w_a, b_a, w_x, b_x, sink, rel_table, w_br_attn, w_br_rnn, w_out):
    B, S, _ = h.shape
    q, k, v, xr, yr, g_attn, g_rnn = jnp.split(h @ w_in, SPLIT_POINTS, axis=-1)
    attn = windowed_attention(q.reshape(B, S, N_HEADS, HEAD_DIM),
                              k.reshape(B, S, N_KV_HEADS, HEAD_DIM),
                              v.reshape(B, S, N_KV_HEADS, HEAD_DIM), sink, rel_table)
    xc = depthwise_conv(xr, conv_w, conv_b).astype(jnp.float32)
    rec = (rg_lru_direction(xc, lam[0], w_a[0], b_a[0], w_x[0], b_x[0], False)
           + rg_lru_direction(xc, lam[1], w_a[1], b_a[1], w_x[1], b_x[1], True))
    rnn = rec.astype(h.dtype) * jax.nn.gelu(yr)
    merged = jax.nn.sigmoid(g_attn) * (attn @ w_br_attn) + jax.nn.sigmoid(g_rnn) * (rnn @ w_br_rnn)
    return merged @ w_out


def trunk(x, ffn1_norm, ffn1_w_up, ffn1_w_down, mix_norm, w_in, conv_w, conv_b, rg_lambda,
          rg_w_a, rg_b_a, rg_w_x, rg_b_x, attn_sink, rel_bias_table, w_br_attn, w_br_rnn, w_out,
          ffn2_norm, ffn2_w_up, ffn2_w_down, final_norm):
    for l in range(DEPTH):
        x = x + 0.5 * swiglu_ffn(rms_norm(x, ffn1_norm[l]), ffn1_w_up[l], ffn1_w_down[l])
        x = x + mixer(rms_norm(x, mix_norm[l]), w_in[l], conv_w[l], conv_b[l], rg_lambda[l],
                      rg_w_a[l], rg_b_a[l], rg_w_x[l], rg_b_x[l], attn_sink[l], rel_bias_table,
                      w_br_attn[l], w_br_rnn[l], w_out[l])
        x = x + 0.5 * swiglu_ffn(rms_norm(x, ffn2_norm[l]), ffn2_w_up[l], ffn2_w_down[l])
    return rms_norm(x, final_norm)


def setup_inputs(seed: int = 0) -> dict:
    key = jax.random.key(seed)
    ks = jax.random.split(key, 26)
    f32 = jnp.float32

    def nrm(k, shape, scale):
        return jax.random.normal(k, shape, f32) * scale

    def gain(k, shape):
        return 1.0 + 0.05 * jax.random.normal(k, shape, f32)

    a0 = jax.random.uniform(ks[7], (DEPTH, 2, D_RNN), f32, 0.9, 0.999)
    return {
        "x_prompt": nrm(ks[0], (BATCH, SEQ, D_MODEL), 1.0),
        "x_sample": nrm(ks[1], (DEC_BATCH, DEC_SEQ, D_MODEL), 1.0),
        "ffn1_norm": gain(ks[2], (DEPTH, D_MODEL)),
        "ffn1_w_up": nrm(ks[3], (DEPTH, D_MODEL, 2 * D_FF), D_MODEL ** -0.5),
        "ffn1_w_down": nrm(ks[4], (DEPTH, D_FF, D_MODEL), D_FF ** -0.5),
        "mix_norm": gain(ks[5], (DEPTH, D_MODEL)),
        "w_in": nrm(ks[6], (DEPTH, D_MODEL, IN_W), D_MODEL ** -0.5),
        "conv_w": nrm(ks[8], (DEPTH, CONV_W, D_RNN), CONV_W ** -0.5),
        "conv_b": nrm(ks[9], (DEPTH, D_RNN), 0.02),
        "rg_lambda": jnp.log(a0) - jnp.log1p(-a0),
        "rg_w_a": nrm(ks[10], (DEPTH, 2, RNN_BLOCKS, RNN_BW, RNN_BW), RNN_BW ** -0.5),
        "rg_b_a": nrm(ks[11], (DEPTH, 2, D_RNN), 0.02),
        "rg_w_x": nrm(ks[12], (DEPTH, 2, RNN_BLOCKS, RNN_BW, RNN_BW), RNN_BW ** -0.5),
        "rg_b_x": nrm(ks[13], (DEPTH, 2, D_RNN), 0.02),
        "attn_sink": nrm(ks[14], (DEPTH, N_HEADS), 0.5),
        "rel_bias_table": nrm(ks[15], (NUM_BUCKETS, N_HEADS), 0.5),
        "w_br_attn": nrm(ks[16], (DEPTH, Q_W, D_MODEL), Q_W ** -0.5),
        "w_br_rnn": nrm(ks[17], (DEPTH, D_RNN, D_MODEL), D_RNN ** -0.5),
        "w_out": nrm(ks[18], (DEPTH, D_MODEL, D_MODEL), D_MODEL ** -0.5),
        "ffn2_norm": gain(ks[19], (DEPTH, D_MODEL)),
        "ffn2_w_up": nrm(ks[20], (DEPTH, D_MODEL, 2 * D_FF), D_MODEL ** -0.5),
        "ffn2_w_down": nrm(ks[21], (DEPTH, D_FF, D_MODEL), D_FF ** -0.5),
        "final_norm": gain(ks[22], (D_MODEL,)),
    }


def reference(x_prompt, x_sample, ffn1_norm, ffn1_w_up, ffn1_w_down, mix_norm, w_in, conv_w, conv_b,
              rg_lambda, rg_w_a, rg_b_a, rg_w_x, rg_b_x, attn_sink, rel_bias_table, w_br_attn,
              w_br_rnn, w_out, ffn2_norm, ffn2_w_up, ffn2_w_down, final_norm):
    y_prompt = trunk(x_prompt, ffn1_norm, ffn1_w_up, ffn1_w_down, mix_norm, w_in, conv_w, conv_b,
                     rg_lambda, rg_w_a, rg_b_a, rg_w_x, rg_b_x, attn_sink, rel_bias_table, w_br_attn,
                     w_br_rnn, w_out, ffn2_norm, ffn2_w_up, ffn2_w_down, final_norm)
    y_sample = trunk(x_sample, ffn1_norm, ffn1_w_up, ffn1_w_down, mix_norm, w_in, conv_w, conv_b,
                     rg_lambda, rg_w_a, rg_b_a, rg_w_x, rg_b_x, attn_sink, rel_bias_table, w_br_attn,
                     w_br_rnn, w_out, ffn2_norm, ffn2_w_up, ffn2_w_down, final_norm)
    return (y_prompt, y_sample)
```

```python
import math
from contextlib import ExitStack

import numpy as np

import concourse.bass as bass
import concourse.mybir as mybir
from concourse.bass_utils import run_bass_kernel_spmd

F32 = mybir.dt.float32
BF16 = mybir.dt.bfloat16
AF = mybir.ActivationFunctionType
ALU = mybir.AluOpType

D = 1024
DFF = 2816
NJ = DFF // 128
T = 512
HALO = 128
WIN_W = T + 2 * HALO
C0, C1 = HALO, HALO + T
NSLOT = 7
EPS = 1e-6
NEG = -30000.0
GELU_C = 1.5957691216057308

P_F1, P_MIX, P_F2, P_CW, P_CB, P_LAM, P_BA, P_BX, P_FIN = 0, 8, 16, 24, 56, 64, 80, 96, 112


def t5_buckets(rel):
    n = 16
    max_exact = 8
    ret = (rel > 0).astype(np.int32) * n
    na = np.abs(rel)
    large = max_exact + (np.log(np.maximum(na, 1) / max_exact) / math.log(128 / max_exact) * (n - max_exact)).astype(np.int32)
    large = np.minimum(large, n - 1)
    return ret + np.where(na < max_exact, na, large)


def make_onehot():
    oh = np.zeros((33, 1024), np.float32)
    m = np.arange(1024)
    rel = m - 512
    inw = np.abs(rel) <= 128
    b = t5_buckets(rel)
    oh[b[inw], m[inw]] = 1.0
    oh[32, ~inw] = NEG
    return oh


class Buf:
    __slots__ = ("name", "w", "r", "sem", "cnt")

    def __init__(self, name, inherit=None):
        self.name = name
        self.w = None
        self.r = dict(inherit) if inherit else {}
        self.sem = None
        self.cnt = 0


class TB:
    __slots__ = ("t", "b")

    def __init__(self, t, b):
        self.t = t
        self.b = b


class Prog:
    def __init__(self, nc, es):
        self.nc = nc
        self.es = es
        self.eng = {"pe": nc.tensor, "act": nc.scalar, "dve": nc.vector, "pool": nc.gpsimd, "sp": nc.sync}
        self.sem = {e: es.enter_context(nc.semaphore("s_" + e)) for e in ("pe", "act", "dve", "pool")}
        self.cnt = {e: 0 for e in self.sem}
        self.seen = {e: {} for e in self.eng}
        self.dsems = {}
        self.dcnt = {}
        self.inherit = {}
        self.banks = []
        self.bank_i = 0
        self.slots = []
        self.slot_i = 0
        self.ninstr = 0

    def buf(self, name, staged=False):
        return Buf(name, self.inherit if staged else None)

    def stage_end(self, bufs):
        for b in bufs:
            if b.w is not None:
                k, v = b.w
                if v > self.inherit.get(k, 0):
                    self.inherit[k] = v
            for k, v in b.r.items():
                if v > self.inherit.get(k, 0):
                    self.inherit[k] = v

    def _semobj(self, k):
        return self.sem[k] if k in self.sem else self.dsems[k]

    def _wait(self, e, toks):
        best = {}
        for t in toks:
            if t is None:
                continue
            k, v = t
            if v > best.get(k, 0):
                best[k] = v
        for k, v in best.items():
            if self.seen[e].get(k, 0) >= v:
                continue
            self.eng[e].wait_ge(self._semobj(k), v)
            self.ninstr += 1
            self.seen[e][k] = v

    def _deps(self, e, reads, writes):
        toks = []
        for b in reads:
            toks.append(b.w)
        for b in writes:
            if b.w is not None and b.w[0] != e:
                toks.append(b.w)
            for k, v in b.r.items():
                if k != e:
                    toks.append((k, v))
        return toks

    def _record(self, tok, reads, writes):
        k, v = tok
        for b in writes:
            b.w = tok
            b.r = {}
        for b in reads:
            if b in writes:
                continue
            if v > b.r.get(k, 0):
                b.r[k] = v

    def op(self, e, fn, reads=(), writes=()):
        self._wait(e, self._deps(e, reads, writes))
        ins = fn(self.eng[e])
        self.cnt[e] += 1
        ins.then_inc(self.sem[e], 1)
        self.ninstr += 1
        tok = (e, self.cnt[e])
        self._record(tok, reads, writes)
        return tok

    def bank(self):
        b = self.banks[self.bank_i % len(self.banks)]
        self.bank_i += 1
        return b

    def mm(self, bank, mms, reads=(), first=True, last=True):
        toks = [b.w for b in reads]
        if first:
            toks += self._deps("pe", (), (bank.b,))
        self._wait("pe", toks)
        n = len(mms)
        ins = None
        for i, m in enumerate(mms):
            if m[0] == "T":
                ins = self.nc.tensor.transpose(out=m[1], in_=m[2], identity=m[3])
            else:
                out_ap, lhsT, rhs, start = m
                ins = self.nc.tensor.matmul(out_ap, lhsT, rhs, start=bool(start), stop=bool(last and i == n - 1))
            self.ninstr += 1
        self.cnt["pe"] += 1
        ins.then_inc(self.sem["pe"], 1)
        tok = ("pe", self.cnt["pe"])
        self._record(tok, reads, (bank.b,) if last else ())
        if not last:
            bank.b.w = tok
            bank.b.r = {}
        return tok

    def dsem(self, buf):
        name = "d_" + buf.name
        if name not in self.dsems:
            self.dsems[name] = self.es.enter_context(self.nc.semaphore(name))
            self.dcnt[name] = 0
        return name

    def dma(self, out_ap, in_ap, sem_buf, reads=(), writes=()):
        self._wait("sp", self._deps("sp", reads, writes))
        name = self.dsem(sem_buf)
        ins = self.nc.sync.dma_start(out=out_ap, in_=in_ap)
        self.dcnt[name] += 16
        ins.then_inc(self.dsems[name], 16)
        self.ninstr += 1
        tok = (name, self.dcnt[name])
        self._record(tok, reads, writes)
        return tok

    def wload(self, dram_ap, n):
        s = self.slots[self.slot_i % len(self.slots)]
        self.slot_i += 1
        self.dma(s.t[:, 0:n], dram_ap, s.b, writes=(s.b,))
        return s


def pairview(s):
    return s.t[:, 0:2048].rearrange("p (k g c) -> p k g c", k=8, g=2)


class Cfg:
    def __init__(self, S=8192, NSEQ=2, DEPTH=4):
        self.S = S
        self.NSEQ = NSEQ
        self.DEPTH = DEPTH
        self.NT = S // T
        self.NB = S // 128
        self.SP = S + 2 * HALO


WEIGHT_NAMES = ["ffn1_norm", "ffn1_w_up", "ffn1_w_down", "mix_norm", "w_in", "conv_w", "conv_b", "rg_lambda",
                "rg_w_a", "rg_b_a", "rg_w_x", "rg_b_x", "attn_sink", "rel_bias_table", "w_br_attn", "w_br_rnn",
                "w_out", "ffn2_norm", "ffn2_w_up", "ffn2_w_down", "final_norm"]
WEIGHT_SHAPES = {
    "ffn1_norm": [4, 1024], "ffn1_w_up": [4, 1024, 5632], "ffn1_w_down": [4, 2816, 1024], "mix_norm": [4, 1024],
    "w_in": [4, 1024, 5632], "conv_w": [4, 4, 1024], "conv_b": [4, 1024], "rg_lambda": [4, 2, 1024],
    "rg_w_a": [4, 2, 8, 128, 128], "rg_b_a": [4, 2, 1024], "rg_w_x": [4, 2, 8, 128, 128], "rg_b_x": [4, 2, 1024],
    "attn_sink": [4, 8], "rel_bias_table": [32, 8], "w_br_attn": [4, 1024, 1024], "w_br_rnn": [4, 1024, 1024],
    "w_out": [4, 1024, 1024], "ffn2_norm": [4, 1024], "ffn2_w_up": [4, 1024, 5632], "ffn2_w_down": [4, 2816, 1024],
    "final_norm": [1024],
}


def build(cfg):
    nc = bass.Bass("TRN2", target_bir_lowering=False)
    S, NSEQ, DEPTH, NT, NB, SPAD = cfg.S, cfg.NSEQ, cfg.DEPTH, cfg.NT, cfg.NB, cfg.SP
    inp = {}
    for n in WEIGHT_NAMES:
        inp[n] = nc.dram_tensor(n, WEIGHT_SHAPES[n], F32, kind="ExternalInput").ap()
    xin = nc.dram_tensor("xin", [NSEQ, S, D], F32, kind="ExternalInput").ap()
    ident_d = nc.dram_tensor("ident", [128, 128], F32, kind="ExternalInput").ap()
    oh_d = nc.dram_tensor("onehot", [33, 1024], F32, kind="ExternalInput").ap()
    yout = nc.dram_tensor("y", [NSEQ, S, D], F32, kind="ExternalOutput").ap()

    xbuf = [nc.dram_tensor(f"xbuf{i}", [8, 128, SPAD], F32, kind="Internal").ap() for i in range(2)]
    hbst = nc.dram_tensor("hbst", [8, 128, S], F32, kind="Internal").ap()
    xcst = nc.dram_tensor("xcst", [8, 128, S], F32, kind="Internal").ap()
    biasF = nc.dram_tensor("biasF", [8, 1024], F32, kind="Internal").ap()
    WUP = [[nc.dram_tensor(f"wup{l}_{f}", [NJ, 128, 2048], BF16, kind="Internal").ap() for f in range(2)] for l in range(DEPTH)]
    WDN = [[nc.dram_tensor(f"wdn{l}_{f}", [16, 128, 1408], BF16, kind="Internal").ap() for f in range(2)] for l in range(DEPTH)]
    WIN = [nc.dram_tensor(f"win{l}", [22, 128, 2048], BF16, kind="Internal").ap() for l in range(DEPTH)]
    WBR = [[nc.dram_tensor(f"wbr{l}_{a}", [4, 128, 2048], BF16, kind="Internal").ap() for a in range(2)] for l in range(DEPTH)]
    WOUT = [nc.dram_tensor(f"wout{l}", [4, 128, 2048], BF16, kind="Internal").ap() for l in range(DEPTH)]
    WG = [nc.dram_tensor(f"wg{l}", [2, 128, 2048], BF16, kind="Internal").ap() for l in range(DEPTH)]

    with ExitStack() as es:
        P = Prog(nc, es)
        E = es.enter_context

        uid = [0]

        def SB(name, shape, dt=F32, stack=None):
            if stack is not None:
                uid[0] += 1
                name = f"{name}_{uid[0]}"
            return (stack or es).enter_context(nc.sbuf_tensor(name, shape, dt))

        identf = SB("identf", [128, 128]); identf_b = P.buf("identf")
        identb = SB("identb", [128, 128], BF16); identb_b = P.buf("identb")
        onesb = SB("onesb", [128, 128], BF16); onesb_b = P.buf("onesb")
        zt = SB("zt", [128, 128]); zt_b = P.buf("zt")
        prm = SB("prm", [128, DEPTH, 120]); prm_b = P.buf("prm")
        cs = SB("cs", [128, DEPTH, 16]); cs_b = P.buf("cs")
        est = SB("es", [128, 32]); es_b = P.buf("es")
        esx = SB("esx", [128, 2, 4, 128]); esx_b = P.buf("esx")
        Bhi = SB("Bhi", [128, 8, 384], BF16); Bhi_b = P.buf("Bhi")
        Blo = SB("Blo", [128, 8, 384], BF16); Blo_b = P.buf("Blo")
        carry = SB("carry", [128, 2, 8]); carry_b = [[P.buf(f"carry{d}_{k}") for k in range(8)] for d in range(2)]
        for i in range(8):
            P.banks.append(TB(E(nc.psum_tensor(f"bank{i}", [128, 512], F32)), P.buf(f"bank{i}")))

        P.dma(identf[:], ident_d[:, :], identf_b, writes=(identf_b,))
        P.op("dve", lambda e: e.tensor_copy(out=identb[:], in_=identf[:]), reads=(identf_b,), writes=(identb_b,))
        P.op("dve", lambda e: e.memset(onesb[:], 1.0), writes=(onesb_b,))
        P.op("dve", lambda e: e.memset(zt[:], 0.0), writes=(zt_b,))
        ztok = []
        for xb_ in xbuf:
            for k in range(8):
                ztok.append(P.dma(xb_[k, :, 0:HALO], zt[:], zt_b, reads=(zt_b,)))
                ztok.append(P.dma(xb_[k, :, HALO + S:SPAD], zt[:], zt_b, reads=(zt_b,)))
        P._wait("sp", ztok[-1:])

        with ExitStack() as ss:
            for l in range(DEPTH):
                stg = SB(f"stg{l}", [128, 128], stack=ss); stg_b = P.buf(f"stg{l}", staged=True)
                rows = [(P_F1, inp["ffn1_norm"][l].rearrange("(k p) -> k p", p=128), 8),
                        (P_MIX, inp["mix_norm"][l].rearrange("(k p) -> k p", p=128), 8),
                        (P_F2, inp["ffn2_norm"][l].rearrange("(k p) -> k p", p=128), 8),
                        (P_CW, inp["conv_w"][l].rearrange("t (k p) -> (t k) p", p=128), 32),
                        (P_CB, inp["conv_b"][l].rearrange("(k p) -> k p", p=128), 8),
                        (P_LAM, inp["rg_lambda"][l].rearrange("d (k p) -> (d k) p", p=128), 16),
                        (P_BA, inp["rg_b_a"][l].rearrange("d (k p) -> (d k) p", p=128), 16),
                        (P_BX, inp["rg_b_x"][l].rearrange("d (k p) -> (d k) p", p=128), 16),
                        (P_FIN, inp["final_norm"].rearrange("(k p) -> k p", p=128), 8)]
                for (r0, ap, n) in rows:
                    P.dma(stg[r0:r0 + n, :], ap, stg_b, writes=(stg_b,))
                bk = P.bank()
                P.mm(bk, [("T", bk.t[:, 0:120], stg[0:120, :], identf[0:120, 0:120])], reads=(stg_b, identf_b))
                P.op("dve", lambda e: e.tensor_copy(out=prm[:, l, :], in_=bk.t[:, 0:120]), reads=(bk.b,), writes=(prm_b,))
                P.stage_end([stg_b])
            for l in range(DEPTH):
                P.op("act", lambda e: e.activation(out=cs[:, l, :], in_=prm[:, l, P_LAM:P_LAM + 16], func=AF.Exp, scale=-1.0),
                     reads=(prm_b,), writes=(cs_b,))
            for l in range(DEPTH):
                P.op("act", lambda e: e.activation(out=cs[:, l, :], in_=cs[:, l, :], func=AF.Ln, bias=1.0),
                     reads=(cs_b,), writes=(cs_b,))
            P.op("dve", lambda e: e.tensor_scalar(out=cs[:], in0=cs[:], scalar1=-8.0, scalar2=None, op0=ALU.mult),
                 reads=(cs_b,), writes=(cs_b,))
            P.dma(est[:], bass.AP(inp["attn_sink"].tensor, 0, [[0, 128], [1, 32]]), es_b, writes=(es_b,))
            P.op("act", lambda e: e.activation(out=est[:], in_=est[:], func=AF.Exp), reads=(es_b,), writes=(es_b,))

            tab = SB("tab", [33, 8], stack=ss); tab_b = P.buf("tab", staged=True)
            ohs = SB("ohs", [33, 1024], stack=ss); ohs_b = P.buf("ohs", staged=True)
            Fs = SB("Fs", [8, 1024], stack=ss); Fs_b = P.buf("Fs", staged=True)
            Tr = SB("Tr", [128, 8, 384], stack=ss); Tr_b = P.buf("Tr", staged=True)
            Bf = SB("Bf", [128, 8, 384], stack=ss); Bf_b = P.buf("Bf", staged=True)
            P.op("dve", lambda e: e.memset(tab[:], 1.0), writes=(tab_b,))
            P.dma(tab[0:32, :], inp["rel_bias_table"][:, :], tab_b, reads=(), writes=(tab_b,))
            P.dma(ohs[:], oh_d[:, :], ohs_b, writes=(ohs_b,))
            for hh in range(2):
                bk = P.bank()
                P.mm(bk, [(bk.t[0:8, :], tab[0:33, 0:8], ohs[0:33, hh * 512:(hh + 1) * 512], True)], reads=(tab_b, ohs_b))
                P.op("dve", lambda e: e.tensor_copy(out=Fs[:, hh * 512:(hh + 1) * 512], in_=bk.t[0:8, :]), reads=(bk.b,), writes=(Fs_b,))
            tk = P.dma(biasF[:, :], Fs[:], Fs_b, reads=(Fs_b,))
            P._wait("sp", [tk])
            for h in range(8):
                P.dma(Tr[:, h, :].rearrange("p (o c) -> p o c", o=3),
                      bass.AP(biasF.tensor, h * 1024 + 257, [[1, 128], [128, 3], [1, 128]]), Tr_b, writes=(Tr_b,))
            for h in range(8):
                for o in range(3):
                    P.op("dve", lambda e: e.tensor_copy(out=Bf[:, h, o * 128:(o + 1) * 128], in_=Tr[:, h, o * 128:(o + 1) * 128][:, ::-1]),
                         reads=(Tr_b,), writes=(Bf_b,))
            P.op("dve", lambda e: e.tensor_copy(out=Bhi[:], in_=Bf[:]), reads=(Bf_b,), writes=(Bhi_b,))
            P.op("dve", lambda e: e.tensor_tensor(out=Bf[:], in0=Bf[:], in1=Bhi[:], op=ALU.subtract), reads=(Bf_b, Bhi_b), writes=(Bf_b,))
            P.op("dve", lambda e: e.tensor_copy(out=Blo[:], in_=Bf[:]), reads=(Bf_b,), writes=(Blo_b,))
            P.stage_end([tab_b, ohs_b, Fs_b, Tr_b, Bf_b])

        with ExitStack() as ss:
            NCB = 2
            CW = 11264
            cf = [TB(SB(f"cf{i}", [128, CW], stack=ss), P.buf(f"cf{i}", staged=True)) for i in range(NCB)]
            cb = [TB(SB(f"cb{i}", [128, CW], BF16, stack=ss), P.buf(f"cb{i}", staged=True)) for i in range(NCB)]
            cast_i = [0]
            last_store = {}
            eng_rr = [0]

            def job(loads, n, ngrp, in_view, out_view, stores):
                i = cast_i[0]
                cast_i[0] += 1
                A = cf[i % NCB]
                Bt = cb[i % NCB]
                for vf, dap in loads:
                    P.dma(vf(A.t), dap, A.b, writes=(A.b,))
                iv = in_view(A.t)
                ov = out_view(Bt.t)
                for gi in range(ngrp):
                    e_ = ("dve", "act", "dve", "pool")[eng_rr[0] % 4]
                    eng_rr[0] += 1
                    if e_ == "act":
                        P.op("act", lambda e: e.activation(out=ov(gi), in_=iv(gi), func=AF.Copy), reads=(A.b,), writes=(Bt.b,))
                    else:
                        P.op(e_, lambda e: e.tensor_copy(out=ov(gi), in_=iv(gi)), reads=(A.b,), writes=(Bt.b,))
                for dap, vf in stores:
                    last_store[Bt.b.name] = P.dma(dap, vf(Bt.t), Bt.b, reads=(Bt.b,))

            for l in range(DEPTH):
                for f, nm in enumerate(("ffn1_w_up", "ffn2_w_up")):
                    wt = inp[nm].tensor
                    base = l * 1024 * 5632
                    for kg in range(4):
                        loads = []
                        for g in range(2):
                            src = bass.AP(wt, base + (kg * 2) * 128 * 5632 + g * DFF, [[5632, 128], [128 * 5632, 2], [1, DFF]])
                            loads.append(((lambda t, g=g: t[:, 0:CW].rearrange("p (kk g c) -> p kk g c", kk=2, g=2)[:, :, g, :]), src))
                        in_view = lambda t: (lambda gi: t[:, gi * DFF:(gi + 1) * DFF].rearrange("p (j c) -> p j c", c=128))
                        out_view = lambda t: (lambda gi: t[:, 0:CW].rearrange("p (j q c) -> p j q c", j=NJ, q=4)[:, :, gi, :])
                        dst = bass.AP(WUP[l][f].tensor, kg * 512, [[2048, 128], [128 * 2048, NJ], [1, 512]])
                        job(loads, CW, 4, in_view, out_view, [(dst, lambda t: t[:, 0:CW].rearrange("p (j r) -> p j r", j=NJ))])
                wt = inp["w_in"].tensor
                base = l * 1024 * 5632
                for kg in range(2):
                    for hh in range(2):
                        src = bass.AP(wt, base + (kg * 4) * 128 * 5632 + hh * 2816, [[5632, 128], [128 * 5632, 4], [1, 2816]])
                        loads = [((lambda t: t[:, 0:CW].rearrange("p (kk c) -> p kk c", kk=4)), src)]
                        in_view = lambda t: (lambda gi: t[:, gi * 2816:(gi + 1) * 2816].rearrange("p (u c) -> p u c", c=256))
                        out_view = lambda t: (lambda gi: t[:, 0:CW].rearrange("p (u q c) -> p u q c", u=11, q=4)[:, :, gi, :])
                        dst = bass.AP(WIN[l].tensor, hh * 11 * 128 * 2048 + kg * 1024, [[2048, 128], [128 * 2048, 11], [1, 1024]])
                        job(loads, CW, 4, in_view, out_view, [(dst, lambda t: t[:, 0:CW].rearrange("p (u r) -> p u r", u=11))])
                for f, nm in enumerate(("ffn1_w_down", "ffn2_w_down")):
                    wt = inp[nm].tensor
                    base = l * DFF * 1024
                    for jh in range(2):
                        src = bass.AP(wt, base + jh * 11 * 128 * 1024, [[1024, 128], [128 * 1024, 11], [1, 1024]])
                        loads = [((lambda t: t[:, 0:CW].rearrange("p (jj c) -> p jj c", jj=11)), src)]
                        in_view = lambda t: (lambda gi: t[:, gi * 1024:(gi + 1) * 1024].rearrange("p (m c) -> p m c", c=128))
                        out_view = lambda t: (lambda gi: t[:, 0:CW].rearrange("p (m jj c) -> p m jj c", m=8, jj=11)[:, :, gi, :])
                        dst = bass.AP(WDN[l][f].tensor, jh * 128 * 1408, [[1408, 128], [2 * 128 * 1408, 8], [1, 1408]])
                        job(loads, CW, 11, in_view, out_view, [(dst, lambda t: t[:, 0:CW].rearrange("p (m r) -> p m r", m=8))])
                for nm, dstT in (("w_br_attn", WBR[l][0]), ("w_br_rnn", WBR[l][1]), ("w_out", WOUT[l])):
                    wt = inp[nm].tensor
                    base = l * 1024 * 1024
                    src = bass.AP(wt, base, [[1024, 128], [128 * 1024, 8], [1, 1024]])
                    loads = [((lambda t: t[:, 0:8192].rearrange("p (kk c) -> p kk c", kk=8)), src)]
                    in_view = lambda t: (lambda gi: t[:, gi * 1024:(gi + 1) * 1024].rearrange("p (u c) -> p u c", c=256))
                    out_view = lambda t: (lambda gi: t[:, 0:8192].rearrange("p (u q c) -> p u q c", u=4, q=8)[:, :, gi, :])
                    dst = bass.AP(dstT.tensor, 0, [[2048, 128], [128 * 2048, 4], [1, 2048]])
                    job(loads, 8192, 8, in_view, out_view, [(dst, lambda t: t[:, 0:8192].rearrange("p (u r) -> p u r", u=4))])
                loads = []
                for gi_, nm in enumerate(("rg_w_a", "rg_w_x")):
                    src = bass.AP(inp[nm].tensor, l * 2 * 8 * 128 * 128, [[128, 128], [128 * 128, 16], [1, 128]])
                    loads.append(((lambda t, gi_=gi_: t[:, gi_ * 2048:(gi_ + 1) * 2048].rearrange("p (b c) -> p b c", c=128)), src))
                in_view = lambda t: (lambda gi: t[:, gi * 2048:(gi + 1) * 2048].rearrange("p (b c) -> p b c", c=128))
                out_view = lambda t: (lambda gi: t[:, 0:4096].rearrange("p (b q c) -> p b q c", b=16, q=2)[:, :, gi, :])
                dst = bass.AP(WG[l].tensor, 0, [[2048, 128], [128 * 2048, 2], [1, 2048]])
                job(loads, 4096, 2, in_view, out_view, [(dst, lambda t: t[:, 0:4096].rearrange("p (d r) -> p d r", d=2))])
            P._wait("sp", list(last_store.values()))
            P.stage_end([x.b for x in cf] + [x.b for x in cb])

        xW = SB("xW", [128, 8, WIN_W]); xw = [P.buf(f"xw{k}") for k in range(8)]
        xR = SB("xR", [128, 8, T]); xr_ = [P.buf(f"xr_{k}") for k in range(8)]
        hT = SB("hT", [128, 8, WIN_W], BF16); hT_b = P.buf("hT")
        rstd = SB("rstd", [128, WIN_W]); rstd_b = P.buf("rstd")
        for i in range(NSLOT):
            P.slots.append(TB(SB(f"slot{i}", [128, 2048], BF16), P.buf(f"slot{i}")))
        attnT = SB("attnT", [128, 8, T], BF16); attn_b = [P.buf(f"attn{g}") for g in range(2)]
        gy = SB("gy", [128, 8, T], BF16); gy_b = [P.buf(f"gy{k}") for k in range(8)]
        mrg = SB("mrg", [128, 8, T], BF16); mrg_b = [P.buf(f"mrg{k}") for k in range(8)]
        mtmp = [TB(SB(f"mtmp{i}", [128, T]), P.buf(f"mtmp{i}")) for i in range(4)]
        xcB = SB("xcB", [128, 8, T]); xcB_b = [P.buf(f"xcB{k}") for k in range(8)]
        xld_b = P.buf("xld")
        xcs_b = P.buf("xcs")
        xcl_b = P.buf("xcl")
        xst_b = P.buf("xst")

        xreg = [[P.buf(f"xreg{i}_{t}") for t in range(NT)] for i in range(2)]
        hbreg = [[P.buf(f"hbreg{t}_{k}") for k in range(8)] for t in range(NT)]
        xcreg = [P.buf(f"xcreg{t}") for t in range(NT)]

        def load_window(pp, i):
            t0 = i * T
            regs = [xreg[pp][j] for j in (i - 1, i, i + 1) if 0 <= j < NT]
            src = bass.AP(xbuf[pp].tensor, t0, [[SPAD, 128], [128 * SPAD, 8], [1, WIN_W]])
            P.dma(xW[:, :, :], src, xld_b, reads=regs, writes=xw)

        def store_center(pp, i):
            t0 = i * T
            dst = bass.AP(xbuf[pp].tensor, HALO + t0, [[SPAD, 128], [128 * SPAD, 8], [1, T]])
            P.dma(dst, xR[:, :, :], xst_b, reads=xr_, writes=(xreg[pp][i],))

        def emit_norm(c0, c1, gcol, out_t, out_b, win=True):
            st_, sb_, off = (xW, xw, 0) if win else (xR, xr_, C0)
            for k in range(8):
                P.op("act", lambda e: e.activation(out=hT[:, k, c0:c1], in_=st_[:, k, c0 - off:c1 - off], func=AF.Square),
                     reads=(sb_[k],), writes=(hT_b,))
            for a in range(c0, c1, 512):
                b = min(a + 512, c1)
                bk = P.bank()
                P.mm(bk, [(bk.t[:, 0:b - a], onesb[:], hT[:, k, a:b], k == 0) for k in range(8)], reads=(hT_b, onesb_b))
                P.op("act", lambda e: e.activation(out=rstd[:, a:b], in_=bk.t[:, 0:b - a], func=AF.Sqrt, scale=1.0 / D, bias=EPS),
                     reads=(bk.b,), writes=(rstd_b,))
            P.op("dve", lambda e: e.reciprocal(out=rstd[:, c0:c1], in_=rstd[:, c0:c1]), reads=(rstd_b,), writes=(rstd_b,))
            for k in range(8):
                oap = out_t[:, k, c0:c1] if out_t is hT else out_t[:, k, c0 - C0:c1 - C0]
                P.op("dve", lambda e: e.scalar_tensor_tensor(out=oap, in0=st_[:, k, c0 - off:c1 - off], scalar=gcol(k),
                                                             in1=rstd[:, c0:c1], op0=ALU.mult, op1=ALU.mult),
                     reads=(sb_[k], rstd_b, prm_b), writes=(out_b,))

        def emit_ffn(l, f, hook=None):
            pc = P_F1 if f == 0 else P_F2
            gcol = lambda k: prm[:, l, pc + k:pc + k + 1]
            with ExitStack() as ss:
                act = SB("act", [128, NJ, T], BF16, stack=ss); act_b = [P.buf(f"act{j}", staged=True) for j in range(NJ)]
                silu = [TB(SB(f"silu{i}", [128, T], stack=ss), P.buf(f"silu{i}", staged=True)) for i in range(2)]
                rsb = [TB(SB(f"rsb{i}", [128, T], stack=ss), P.buf(f"rsb{i}", staged=True)) for i in range(2)]
                sqs = SB("sqs", [128, 8, T], BF16, stack=ss); sqs_b = P.buf("sqs", staged=True)
                for k in range(8):
                    e_ = "pool" if k in (2, 5, 7) else "dve"
                    P.op(e_, lambda e: e.tensor_scalar(out=hT[:, k, C0:C1], in0=xR[:, k, :], scalar1=gcol(k), scalar2=1.0, op0=ALU.mult, op1=ALU.mult),
                         reads=(xr_[k], prm_b), writes=(hT_b,))
                for k in range(8):
                    P.op("act", lambda e: e.activation(out=sqs[:, k, :], in_=xR[:, k, :], func=AF.Square), reads=(xr_[k],), writes=(sqs_b,))
                for j in range(NJ):
                    s = P.wload(WUP[l][f][j], 2048)
                    if j == 3 and hook is not None:
                        hook()
                    wv = pairview(s)
                    bg = P.bank(); bu = P.bank()
                    P.mm(bg, [(bg.t[:, :], wv[:, k, 0, :], hT[:, k, C0:C1], k == 0) for k in range(8)], reads=(hT_b, s.b))
                    P.mm(bu, [(bu.t[:, :], wv[:, k, 1, :], hT[:, k, C0:C1], k == 0) for k in range(8)], reads=(hT_b, s.b))
                    if j == 0:
                        bk = P.bank()
                        P.mm(bk, [(bk.t[:, :], onesb[:], sqs[:, k, :], k == 0) for k in range(8)], reads=(sqs_b, onesb_b))
                        P.op("act", lambda e: e.activation(out=rstd[:, C0:C1], in_=bk.t[:, :], func=AF.Sqrt, scale=1.0 / D, bias=EPS),
                             reads=(bk.b,), writes=(rstd_b,))
                        P.op("dve", lambda e: e.reciprocal(out=rstd[:, C0:C1], in_=rstd[:, C0:C1]), reads=(rstd_b,), writes=(rstd_b,))
                    tm_ = silu[j % 2]
                    rs_ = rsb[j % 2]
                    P.op("dve", lambda e: e.tensor_tensor(out=tm_.t[:], in0=bg.t[:, :], in1=rstd[:, C0:C1], op=ALU.mult),
                         reads=(bg.b, rstd_b), writes=(tm_.b,))
                    P.op("act", lambda e: e.activation(out=tm_.t[:], in_=tm_.t[:], func=AF.Silu), reads=(tm_.b,), writes=(tm_.b,))
                    P.op("pool", lambda e: e.tensor_tensor(out=rs_.t[:], in0=tm_.t[:], in1=rstd[:, C0:C1], op=ALU.mult),
                         reads=(tm_.b, rstd_b), writes=(rs_.b,))
                    P.op("dve", lambda e: e.tensor_tensor(out=act[:, j, :], in0=bu.t[:, :], in1=rs_.t[:], op=ALU.mult),
                         reads=(bu.b, rs_.b), writes=(act_b[j],))
                for m in range(8):
                    bo = P.bank()
                    for jh in range(2):
                        s = P.wload(WDN[l][f][m * 2 + jh], 1408)
                        wv = s.t[:, 0:1408].rearrange("p (j c) -> p j c", j=11)
                        P.mm(bo, [(bo.t[:, :], wv[:, jj, :], act[:, jh * 11 + jj, :], (jh == 0 and jj == 0)) for jj in range(11)],
                             reads=tuple(act_b[jh * 11:(jh + 1) * 11]) + (s.b,), first=(jh == 0), last=(jh == 1))
                    P.op("dve", lambda e: e.scalar_tensor_tensor(out=xR[:, m, :], in0=bo.t[:, :], scalar=0.5, in1=xR[:, m, :],
                                                                 op0=ALU.mult, op1=ALU.add),
                         reads=(bo.b, xr_[m]), writes=(xr_[m],))
                P.stage_end(act_b + [x.b for x in silu] + [x.b for x in rsb] + [sqs_b])

        def emit_xr_conv(l, ss, hook=None, stats=None):
            xr4 = SB("xr", [128, 4, 516], stack=ss); xr_b4 = [P.buf(f"xr{k}", staged=True) for k in range(4)]

            class _XR:
                def __getitem__(self, key):
                    p_, kk_, c_ = key
                    return xr4[p_, ((kk_ // 2) % 2) * 2 + (kk_ % 2), c_]
            xr = _XR()
            xr_b = [xr_b4[((k // 2) % 2) * 2 + (k % 2)] for k in range(8)]
            xc = xcB; xc_b = xcB_b
            xcbf = SB("xcbf", [128, 8, T], BF16, stack=ss); xcbf_b = [P.buf(f"xcbf{k}", staged=True) for k in range(8)]
            slots_ = [P.wload(WIN[l][6 + u], 2048) for u in range(4)]
            if hook is not None:
                hook()
            for u in range(4):
                s = slots_[u]
                wv = pairview(s)
                for g in range(2):
                    kk = 2 * u + g
                    bm = P.bank(); be = P.bank()
                    P.mm(bm, [(bm.t[:, :], wv[:, k, g, :], hT[:, k, C0:C1], k == 0) for k in range(8)], reads=(hT_b, s.b))
                    edges = []
                    for k in range(8):
                        edges.append((be.t[:, 0:2], wv[:, k, g, :], hT[:, k, C0 - 2:C0], k == 0))
                        edges.append((be.t[:, 2:3], wv[:, k, g, :], hT[:, k, C1:C1 + 1], False))
                    P.mm(be, edges, reads=(hT_b, s.b))
                    if stats is not None:
                        if kk == 0:
                            stats()
                        P.op("dve", lambda e: e.tensor_tensor(out=xr[:, kk, 2:514], in0=bm.t[:, :], in1=rstd[:, C0:C1], op=ALU.mult),
                             reads=(bm.b, rstd_b), writes=(xr_b[kk],))
                        P.op("dve", lambda e: e.tensor_tensor(out=xr[:, kk, 0:2], in0=be.t[:, 0:2], in1=rstd[:, C0 - 2:C0], op=ALU.mult),
                             reads=(be.b, rstd_b), writes=(xr_b[kk],))
                        P.op("dve", lambda e: e.tensor_tensor(out=xr[:, kk, 514:515], in0=be.t[:, 2:3], in1=rstd[:, C1:C1 + 1], op=ALU.mult),
                             reads=(be.b, rstd_b), writes=(xr_b[kk],))
                        continue
                    P.op("act", lambda e: e.activation(out=xr[:, kk, 2:514], in_=bm.t[:, :], func=AF.Copy), reads=(bm.b,), writes=(xr_b[kk],))
                    P.op("act", lambda e: e.activation(out=xr[:, kk, 0:2], in_=be.t[:, 0:2], func=AF.Copy), reads=(be.b,), writes=(xr_b[kk],))
                    P.op("act", lambda e: e.activation(out=xr[:, kk, 514:515], in_=be.t[:, 2:3], func=AF.Copy), reads=(be.b,), writes=(xr_b[kk],))
                cw = lambda tap, kk: prm[:, l, P_CW + tap * 8 + kk:P_CW + tap * 8 + kk + 1]
                for g in range(2):
                    kk = 2 * u + g
                    P.op("pool", lambda e: e.tensor_scalar(out=xc[:, kk, :], in0=xr[:, kk, 0:T], scalar1=cw(0, kk),
                                                           scalar2=prm[:, l, P_CB + kk:P_CB + kk + 1], op0=ALU.mult, op1=ALU.add),
                         reads=(xr_b[kk], prm_b), writes=(xc_b[kk],))
                for tap in range(1, 4):
                    for g in range(2):
                        kk = 2 * u + g
                        P.op("dve", lambda e: e.scalar_tensor_tensor(out=xc[:, kk, :], in0=xr[:, kk, tap:tap + T], scalar=cw(tap, kk),
                                                                     in1=xc[:, kk, :], op0=ALU.mult, op1=ALU.add),
                             reads=(xr_b[kk], xc_b[kk], prm_b), writes=(xc_b[kk],))
                for g in range(2):
                    kk = 2 * u + g
                    P.op("dve", lambda e: e.tensor_copy(out=xcbf[:, kk, :], in_=xc[:, kk, :]), reads=(xc_b[kk],), writes=(xcbf_b[kk],))
            return dict(xr_b=xr_b4, xc=xc, xc_b=xc_b, xcbf=xcbf, xcbf_b=xcbf_b)

        def emit_chain_batch(l, d, kks, X, gslot, Rs, Hfor, post):
            gv = gslot.t[:, 0:2048].rearrange("p (b g c) -> p b g c", b=8, g=2)
            for j, kk in enumerate(kks):
                R = Rs[j]
                col = d * 8 + kk
                ba = P.bank(); bx = P.bank()
                P.mm(ba, [(ba.t[:, :], gv[:, kk, 0, :], X["xcbf"][:, kk, :], True)], reads=(X["xcbf_b"][kk], gslot.b))
                P.mm(bx, [(bx.t[:, :], gv[:, kk, 1, :], X["xcbf"][:, kk, :], True)], reads=(X["xcbf_b"][kk], gslot.b))
                P.op("act", lambda e: e.activation(out=R["r"].t[:], in_=ba.t[:, :], func=AF.Sigmoid, bias=prm[:, l, P_BA + col:P_BA + col + 1]),
                     reads=(ba.b, prm_b), writes=(R["r"].b,))
                P.op("act", lambda e: e.activation(out=R["i"].t[:], in_=bx.t[:, :], func=AF.Sigmoid, bias=prm[:, l, P_BX + col:P_BX + col + 1]),
                     reads=(bx.b, prm_b), writes=(R["i"].b,))
            for j, kk in enumerate(kks):
                R = Rs[j]
                col = d * 8 + kk
                P.op("act", lambda e: e.activation(out=R["r"].t[:], in_=R["r"].t[:], func=AF.Exp, scale=cs[:, l, col:col + 1]),
                     reads=(R["r"].b, cs_b), writes=(R["r"].b,))
                P.op("pool", lambda e: e.tensor_tensor(out=R["i"].t[:], in0=R["i"].t[:], in1=X["xc"][:, kk, :], op=ALU.mult),
                     reads=(R["i"].b, X["xc_b"][kk]), writes=(R["i"].b,))
            for j, kk in enumerate(kks):
                R = Rs[j]
                P.op("act", lambda e: e.activation(out=R["s"].t[:], in_=R["r"].t[:], func=AF.Square),
                     reads=(R["r"].b,), writes=(R["s"].b,))
            for j, kk in enumerate(kks):
                R = Rs[j]
                P.op("act", lambda e: e.activation(out=R["s"].t[:], in_=R["s"].t[:], func=AF.Sqrt, scale=-1.0, bias=1.0),
                     reads=(R["s"].b,), writes=(R["s"].b,))
            for j, kk in enumerate(kks):
                R = Rs[j]
                P.op("pool", lambda e: e.tensor_tensor(out=R["i"].t[:], in0=R["i"].t[:], in1=R["s"].t[:], op=ALU.mult),
                     reads=(R["i"].b, R["s"].b), writes=(R["i"].b,))
            for j, kk in enumerate(kks):
                R = Rs[j]
                Hh = Hfor(kk)
                cb_ = carry_b[d][kk]
                if d == 0:
                    P.op("dve", lambda e: e.tensor_tensor_scan(out=Hh.t[:], data0=R["r"].t[:], data1=R["i"].t[:], initial=carry[:, d, kk:kk + 1],
                                                               op0=ALU.mult, op1=ALU.add),
                         reads=(R["r"].b, R["i"].b, cb_), writes=(Hh.b,))
                    P.op("dve", lambda e: e.tensor_copy(out=carry[:, d, kk:kk + 1], in_=Hh.t[:, T - 1:T]), reads=(Hh.b,), writes=(cb_,))
                else:
                    P.op("dve", lambda e: e.tensor_tensor_scan(out=Hh.t[:, ::-1], data0=R["r"].t[:, ::-1], data1=R["i"].t[:, ::-1],
                                                               initial=carry[:, d, kk:kk + 1], op0=ALU.mult, op1=ALU.add),
                         reads=(R["r"].b, R["i"].b, cb_), writes=(Hh.b,))
                    P.op("dve", lambda e: e.tensor_copy(out=carry[:, d, kk:kk + 1], in_=Hh.t[:, 0:1]), reads=(Hh.b,), writes=(cb_,))
                post(kk, Hh)

        def rnn_tmps(ss, nR, nH):
            Rs = [dict(r=TB(SB(f"rr{i}", [128, T], stack=ss), P.buf(f"rr{i}", staged=True)),
                       i=TB(SB(f"ri{i}", [128, T], stack=ss), P.buf(f"ri{i}", staged=True)),
                       s=TB(SB(f"rs{i}", [128, T], stack=ss), P.buf(f"rs{i}", staged=True))) for i in range(nR)]
            Hs = [TB(SB(f"H{i}", [128, T], stack=ss), P.buf(f"H{i}", staged=True)) for i in range(nH)]
            return Rs, Hs

        def visit_A(l, i, first, nxt):
            t0 = i * T
            pp = l % 2
            if first:
                load_window(pp, i)
            a0, a1 = C0 - 2, C1 + 1
            for k in range(8):
                e_ = "pool" if k in (2, 5, 7) else "dve"
                P.op(e_, lambda e: e.tensor_scalar(out=hT[:, k, a0:a1], in0=xW[:, k, a0:a1], scalar1=prm[:, l, P_MIX + k:P_MIX + k + 1],
                                                   scalar2=1.0, op0=ALU.mult, op1=ALU.mult),
                     reads=(xw[k], prm_b), writes=(hT_b,))
            for k in range(8):
                P.op("act", lambda e: e.activation(out=mrg[:, k, :], in_=xW[:, k, a0:a0 + 512], func=AF.Square), reads=(xw[k],), writes=(mrg_b[k],))
                P.op("act", lambda e: e.activation(out=attnT[:, k, 0:3], in_=xW[:, k, a0 + 512:a1], func=AF.Square), reads=(xw[k],), writes=(attn_b[k // 4],))

            def stats():
                bk = P.bank()
                P.mm(bk, [(bk.t[:, :], onesb[:], mrg[:, k, :], k == 0) for k in range(8)], reads=tuple(mrg_b) + (onesb_b,))
                P.op("act", lambda e: e.activation(out=rstd[:, a0:a0 + 512], in_=bk.t[:, :], func=AF.Sqrt, scale=1.0 / D, bias=EPS),
                     reads=(bk.b,), writes=(rstd_b,))
                bk2 = P.bank()
                P.mm(bk2, [(bk2.t[:, 0:3], onesb[:], attnT[:, k, 0:3], k == 0) for k in range(8)], reads=(attn_b[0], attn_b[1], onesb_b))
                P.op("act", lambda e: e.activation(out=rstd[:, a0 + 512:a1], in_=bk2.t[:, 0:3], func=AF.Sqrt, scale=1.0 / D, bias=EPS),
                     reads=(bk2.b,), writes=(rstd_b,))
                P.op("dve", lambda e: e.reciprocal(out=rstd[:, a0:a1], in_=rstd[:, a0:a1]), reads=(rstd_b,), writes=(rstd_b,))
            with ExitStack() as ss:
                gs = [None]

                def hook():
                    gs[0] = P.wload(WG[l][1], 2048)
                    if nxt is not None:
                        load_window(pp, nxt)
                X = emit_xr_conv(l, ss, hook, stats)
                P.dma(bass.AP(xcst.tensor, t0, [[S, 128], [128 * S, 8], [1, T]]), X["xc"][:, :, :], xcs_b, reads=X["xc_b"], writes=(xcreg[i],))
                Rs, Hs = rnn_tmps(ss, 4, 4)

                def post(kk, Hh):
                    P.dma(hbst[kk, :, t0:t0 + T], Hh.t[:], Hh.b, reads=(Hh.b,), writes=(hbreg[i][kk],))
                for bi in range(2):
                    emit_chain_batch(l, 1, list(range(bi * 4, bi * 4 + 4)), X, gs[0], Rs, lambda kk: Hs[kk % 4], post)
                allb = X["xr_b"] + X["xc_b"] + X["xcbf_b"] + [h.b for h in Hs]
                for R in Rs:
                    allb += [R["r"].b, R["i"].b, R["s"].b]
                P.stage_end(allb)

        def emit_attention(l, i):
            def valid(wb):
                ab = i * 4 + wb - 1
                return 0 <= ab < NB
            with ExitStack() as ss:
                qT = SB("qT", [128, 8, T], BF16, stack=ss); qT_b = [P.buf(f"q{c}", staged=True) for c in range(8)]
                kT = SB("kT", [128, 2, WIN_W], BF16, stack=ss); kT_b = P.buf("kT", staged=True)
                Vt = SB("Vt", [128, 6, 256], BF16, stack=ss); Vt_b = P.buf("Vt", staged=True)
                PT = [TB(SB(f"PT{j}", [128, 8, 384], BF16, stack=ss), P.buf(f"PT{j}", staged=True)) for j in range(2)]
                rcp = [TB(SB(f"rcp{j}", [128, T], stack=ss), P.buf(f"rcp{j}", staged=True)) for j in range(2)]
                s = P.wload(WIN[l][4], 2048); wv = pairview(s)
                for g in range(2):
                    for (a, b) in ((0, 512), (512, WIN_W)):
                        bk = P.bank()
                        P.mm(bk, [(bk.t[:, 0:b - a], wv[:, k, g, :], hT[:, k, a:b], k == 0) for k in range(8)], reads=(hT_b, s.b))
                        P.op("dve", lambda e: e.tensor_copy(out=kT[:, g, a:b], in_=bk.t[:, 0:b - a]), reads=(bk.b,), writes=(kT_b,))
                s = P.wload(WIN[l][5], 2048); wv = pairview(s)
                for wb in range(6):
                    if not valid(wb):
                        continue
                    bk = P.bank()
                    P.mm(bk, [(bk.t[:, 0:256], hT[:, k, wb * 128:(wb + 1) * 128], wv[:, k, :, :], k == 0) for k in range(8)], reads=(hT_b, s.b))
                    P.op("act", lambda e: e.activation(out=Vt[:, wb, :], in_=bk.t[:, 0:256], func=AF.Copy), reads=(bk.b,), writes=(Vt_b,))
                for u in range(4):
                    s = P.wload(WIN[l][u], 2048); wv = pairview(s)
                    for g in range(2):
                        c = 2 * u + g
                        bk = P.bank()
                        P.mm(bk, [(bk.t[:, :], wv[:, k, g, :], hT[:, k, C0:C1], k == 0) for k in range(8)], reads=(hT_b, s.b))
                        P.op("act", lambda e: e.activation(out=qT[:, c, :], in_=bk.t[:, :], func=AF.Copy, scale=128.0 ** -0.5),
                             reads=(bk.b,), writes=(qT_b[c],))

                def scores(qb):
                    slot = PT[qb % 2]
                    ois = [oi for oi in range(3) if valid(qb + oi)]
                    o0, o1 = ois[0], ois[-1] + 1
                    for h in range(8):
                        g = h // 4
                        bk = P.bank()
                        mms = []
                        for n_, oi in enumerate(ois):
                            wb = qb + oi
                            mms.append((bk.t[:, oi * 128:(oi + 1) * 128], kT[:, g, wb * 128:(wb + 1) * 128], qT[:, h, qb * 128:(qb + 1) * 128], n_ == 0))
                        mms.append((bk.t[:, o0 * 128:o1 * 128], identb[:], Bhi[:, h, o0 * 128:o1 * 128], False))
                        mms.append((bk.t[:, o0 * 128:o1 * 128], identb[:], Blo[:, h, o0 * 128:o1 * 128], False))
                        P.mm(bk, mms, reads=(kT_b, qT_b[h], identb_b, Bhi_b, Blo_b))
                        P.op("act", lambda e: e.activation(out=slot.t[:, h, o0 * 128:o1 * 128], in_=bk.t[:, o0 * 128:o1 * 128], func=AF.Exp),
                             reads=(bk.b,), writes=(slot.b,))

                def pv(qb):
                    slot = PT[qb % 2]
                    ois = [oi for oi in range(3) if valid(qb + oi)]
                    for g in range(2):
                        bo = P.bank(); bd = P.bank()
                        rhs = lambda oi: slot.t[:, 4 * g:4 * g + 4, oi * 128:(oi + 1) * 128]
                        P.mm(bo, [(bo.t[:, :], Vt[:, qb + oi, g * 128:(g + 1) * 128], rhs(oi), n_ == 0) for n_, oi in enumerate(ois)],
                             reads=(Vt_b, slot.b))
                        P.mm(bd, [(bd.t[:, :], onesb[:], rhs(oi), n_ == 0) for n_, oi in enumerate(ois)], reads=(onesb_b, slot.b))
                        rc = rcp[g]
                        P.op("dve", lambda e: e.tensor_tensor(out=rc.t[:], in0=bd.t[:, :], in1=esx[:, g, :, :].rearrange("p h c -> p (h c)"), op=ALU.add),
                             reads=(bd.b, esx_b), writes=(rc.b,))
                        P.op("dve", lambda e: e.reciprocal(out=rc.t[:], in_=rc.t[:]), reads=(rc.b,), writes=(rc.b,))
                        P.op("dve", lambda e: e.tensor_tensor(out=attnT[:, 4 * g:4 * g + 4, qb * 128:(qb + 1) * 128],
                                                              in0=bo.t[:, :].rearrange("p (h c) -> p h c", h=4),
                                                              in1=rc.t[:].rearrange("p (h c) -> p h c", h=4), op=ALU.mult),
                             reads=(bo.b, rc.b), writes=(attn_b[g],))

                scores(0); scores(1); pv(0); scores(2); pv(1); scores(3); pv(2); pv(3)
                P.stage_end(qT_b + [kT_b, Vt_b, PT[0].b, PT[1].b, rcp[0].b, rcp[1].b])

        def emit_rnn_B(l, i):
            t0 = i * T
            with ExitStack() as ss:
                xc = xcB; xc_b = xcB_b
                xcbf = SB("xcbf", [128, 8, T], BF16, stack=ss); xcbf_b = [P.buf(f"xcbf{k}", staged=True) for k in range(8)]
                X = dict(xr_b=[], xc=xc, xc_b=xc_b, xcbf=xcbf, xcbf_b=xcbf_b)
                Rs, Hs = rnn_tmps(ss, 4, 2)
                HB = [TB(SB(f"HB{j}", [128, T], stack=ss), P.buf(f"HB{j}", staged=True)) for j in range(4)]
                G = [TB(SB(f"G{j}", [128, T], stack=ss), P.buf(f"G{j}", staged=True)) for j in range(2)]
                for u in range(4):
                    s = P.wload(WIN[l][10 + u], 2048); wv = pairview(s)
                    for g in range(2):
                        kk = 2 * u + g
                        by = P.bank()
                        Gt = G[kk % 2]
                        P.mm(by, [(by.t[:, :], wv[:, k, g, :], hT[:, k, C0:C1], k == 0) for k in range(8)], reads=(hT_b, s.b))
                        P.op("act", lambda e: e.activation(out=Gt.t[:], in_=by.t[:, :], func=AF.Square), reads=(by.b,), writes=(Gt.b,))
                        P.op("pool", lambda e: e.tensor_scalar(out=Gt.t[:], in0=Gt.t[:], scalar1=0.044715, scalar2=1.0, op0=ALU.mult, op1=ALU.add),
                             reads=(Gt.b,), writes=(Gt.b,))
                        P.op("dve", lambda e: e.tensor_tensor(out=Gt.t[:], in0=Gt.t[:], in1=by.t[:, :], op=ALU.mult), reads=(Gt.b, by.b), writes=(Gt.b,))
                        P.op("act", lambda e: e.activation(out=Gt.t[:], in_=Gt.t[:], func=AF.Sigmoid, scale=GELU_C), reads=(Gt.b,), writes=(Gt.b,))
                        P.op("dve", lambda e: e.tensor_tensor(out=gy[:, kk, :], in0=Gt.t[:], in1=by.t[:, :], op=ALU.mult),
                             reads=(Gt.b, by.b), writes=(gy_b[kk],))
                gslot = P.wload(WG[l][0], 2048)
                for kk in range(8):
                    P.op("dve" if kk % 2 == 0 else "pool", lambda e: e.tensor_copy(out=xcbf[:, kk, :], in_=xc[:, kk, :]),
                         reads=(xc_b[kk],), writes=(xcbf_b[kk],))

                def post(kk, Hh):
                    hb_ = HB[kk % 4]
                    P.op("dve", lambda e: e.tensor_tensor(out=Hh.t[:], in0=Hh.t[:], in1=hb_.t[:], op=ALU.add), reads=(Hh.b, hb_.b), writes=(Hh.b,))
                    P.op("pool", lambda e: e.tensor_tensor(out=gy[:, kk, :], in0=Hh.t[:], in1=gy[:, kk, :], op=ALU.mult),
                         reads=(Hh.b, gy_b[kk]), writes=(gy_b[kk],))
                for bi in range(2):
                    kks = list(range(bi * 4, bi * 4 + 4))
                    for kk in kks:
                        hb_ = HB[kk % 4]
                        P.dma(hb_.t[:], hbst[kk, :, t0:t0 + T], hb_.b, reads=(hbreg[i][kk],), writes=(hb_.b,))
                    emit_chain_batch(l, 0, kks, X, gslot, Rs, lambda kk: Hs[kk % 2], post)
                    emit_merge_A(l, (2 * bi, 2 * bi + 1))
                allb = X["xr_b"] + X["xc_b"] + X["xcbf_b"] + [h.b for h in Hs] + [h.b for h in HB] + [G[0].b, G[1].b]
                for R in Rs:
                    allb += [R["r"].b, R["i"].b, R["s"].b]
                P.stage_end(allb)

        def emit_merge_A(l, us):
            for u in us:
                sga = P.wload(WIN[l][14 + u], 2048); sba = P.wload(WBR[l][0][u], 2048)
                for g in range(2):
                    m = 2 * u + g
                    A = mtmp[(m % 2) * 2]
                    b1 = P.bank(); b2 = P.bank()
                    P.mm(b1, [(b1.t[:, :], pairview(sga)[:, k, g, :], hT[:, k, C0:C1], k == 0) for k in range(8)], reads=(hT_b, sga.b))
                    P.mm(b2, [(b2.t[:, :], pairview(sba)[:, k, g, :], attnT[:, k, :], k == 0) for k in range(8)], reads=(attn_b[0], attn_b[1], sba.b))
                    P.op("act", lambda e: e.activation(out=A.t[:], in_=b1.t[:, :], func=AF.Sigmoid), reads=(b1.b,), writes=(A.b,))
                    P.op("dve", lambda e: e.tensor_tensor(out=mrg[:, m, :], in0=A.t[:], in1=b2.t[:, :], op=ALU.mult), reads=(A.b, b2.b), writes=(mrg_b[m],))

        def emit_merge(l):
            for u in range(4):
                sgr = P.wload(WIN[l][18 + u], 2048); sbr = P.wload(WBR[l][1][u], 2048)
                for g in range(2):
                    m = 2 * u + g
                    Bt = mtmp[(m % 2) * 2 + 1]
                    b3 = P.bank(); b4 = P.bank()
                    P.mm(b3, [(b3.t[:, :], pairview(sgr)[:, k, g, :], hT[:, k, C0:C1], k == 0) for k in range(8)], reads=(hT_b, sgr.b))
                    P.mm(b4, [(b4.t[:, :], pairview(sbr)[:, k, g, :], gy[:, k, :], k == 0) for k in range(8)], reads=tuple(gy_b) + (sbr.b,))
                    P.op("act", lambda e: e.activation(out=Bt.t[:], in_=b3.t[:, :], func=AF.Sigmoid), reads=(b3.b,), writes=(Bt.b,))
                    P.op("dve", lambda e: e.tensor_tensor(out=Bt.t[:], in0=Bt.t[:], in1=b4.t[:, :], op=ALU.mult), reads=(Bt.b, b4.b), writes=(Bt.b,))
                    P.op("pool", lambda e: e.tensor_tensor(out=mrg[:, m, :], in0=mrg[:, m, :], in1=Bt.t[:], op=ALU.add), reads=(mrg_b[m], Bt.b), writes=(mrg_b[m],))
            for u in range(4):
                s = P.wload(WOUT[l][u], 2048); wv = pairview(s)
                for g in range(2):
                    m = 2 * u + g
                    bo = P.bank()
                    P.mm(bo, [(bo.t[:, :], wv[:, k, g, :], mrg[:, k, :], k == 0) for k in range(8)], reads=tuple(mrg_b) + (s.b,))
                    P.op("dve", lambda e: e.tensor_tensor(out=xR[:, m, :], in0=xW[:, m, C0:C1], in1=bo.t[:, :], op=ALU.add),
                         reads=(xw[m], bo.b), writes=(xr_[m],))

        def build_esx(l):
            for h in range(8):
                P.op("dve", lambda e: e.tensor_scalar(out=esx[:, h // 4, h % 4, :], in0=zt[:], scalar1=est[:, l * 8 + h:l * 8 + h + 1],
                                                      scalar2=None, op0=ALU.add),
                     reads=(zt_b, es_b), writes=(esx_b,))

        def visit_0(sq, i):
            t0 = i * T
            with ExitStack() as ss:
                tm = SB("tm", [128, 4, D], stack=ss); tm_b = P.buf("tm", staged=True)
                src = bass.AP(xin.tensor, (sq * S + t0) * D, [[D, 128], [128 * D, 4], [1, D]])
                P.dma(tm[:, :, :], src, tm_b, writes=(tm_b,))
                for k in range(8):
                    bk = P.bank()
                    P.mm(bk, [("T", bk.t[:, b * 128:(b + 1) * 128], tm[:, b, k * 128:(k + 1) * 128], identf[:]) for b in range(4)],
                         reads=(tm_b, identf_b))
                    eng_ = "dve" if k % 2 == 0 else "act"
                    if eng_ == "dve":
                        P.op("dve", lambda e: e.tensor_copy(out=xR[:, k, :], in_=bk.t[:, :]), reads=(bk.b,), writes=(xr_[k],))
                    else:
                        P.op("act", lambda e: e.activation(out=xR[:, k, :], in_=bk.t[:, :], func=AF.Copy), reads=(bk.b,), writes=(xr_[k],))
                P.stage_end([tm_b])
            emit_ffn(0, 0)
            store_center(0, i)

        ystore_toks = {}

        def emit_final(sq, i):
            t0 = i * T
            with ExitStack() as ss:
                yT = SB("yT", [128, 8, T], stack=ss); yT_b = P.buf("yT", staged=True)
                tm2 = SB("tm2", [128, 4, D], stack=ss); tm2_b = P.buf("tm2", staged=True)
                emit_norm_final(yT, yT_b)
                for b in range(4):
                    for kq in range(2):
                        bk = P.bank()
                        P.mm(bk, [("T", bk.t[:, kk * 128:(kk + 1) * 128], yT[:, kq * 4 + kk, b * 128:(b + 1) * 128], identf[:]) for kk in range(4)],
                             reads=(yT_b, identf_b))
                        if kq == 0:
                            P.op("dve", lambda e: e.tensor_copy(out=tm2[:, b, kq * 512:(kq + 1) * 512], in_=bk.t[:, :]), reads=(bk.b,), writes=(tm2_b,))
                        else:
                            P.op("act", lambda e: e.activation(out=tm2[:, b, kq * 512:(kq + 1) * 512], in_=bk.t[:, :], func=AF.Copy),
                                 reads=(bk.b,), writes=(tm2_b,))
                dst = bass.AP(yout.tensor, (sq * S + t0) * D, [[D, 128], [128 * D, 4], [1, D]])
                ystore_toks["y"] = P.dma(dst, tm2[:, :, :], tm2_b, reads=(tm2_b,))
                P.stage_end([yT_b, tm2_b])

        def emit_norm_final(yT, yT_b):
            for k in range(8):
                P.op("act", lambda e: e.activation(out=hT[:, k, C0:C1], in_=xR[:, k, :], func=AF.Square), reads=(xr_[k],), writes=(hT_b,))
            bk = P.bank()
            P.mm(bk, [(bk.t[:, :], onesb[:], hT[:, k, C0:C1], k == 0) for k in range(8)], reads=(hT_b, onesb_b))
            P.op("act", lambda e: e.activation(out=rstd[:, C0:C1], in_=bk.t[:, :], func=AF.Sqrt, scale=1.0 / D, bias=EPS), reads=(bk.b,), writes=(rstd_b,))
            P.op("dve", lambda e: e.reciprocal(out=rstd[:, C0:C1], in_=rstd[:, C0:C1]), reads=(rstd_b,), writes=(rstd_b,))
            for k in range(8):
                P.op("dve", lambda e: e.scalar_tensor_tensor(out=yT[:, k, :], in0=xR[:, k, :], scalar=prm[:, 0, P_FIN + k:P_FIN + k + 1],
                                                             in1=rstd[:, C0:C1], op0=ALU.mult, op1=ALU.mult),
                     reads=(xr_[k], rstd_b, prm_b), writes=(yT_b,))

        def visit_B(sq, l, i):
            pp = l % 2
            if i == 0:
                load_window(pp, i)
            P.dma(xcB[:, :, :], bass.AP(xcst.tensor, i * T, [[S, 128], [128 * S, 8], [1, T]]), xcl_b, reads=(xcreg[i],), writes=xcB_b)
            emit_norm(0, WIN_W, lambda k: prm[:, l, P_MIX + k:P_MIX + k + 1], hT, hT_b)
            emit_attention(l, i)
            emit_rnn_B(l, i)
            emit_merge(l)
            hook = (lambda: load_window(pp, i + 1)) if i + 1 < NT else None
            emit_ffn(l, 1, hook)
            if l + 1 < DEPTH:
                emit_ffn(l + 1, 0)
                store_center((l + 1) % 2, i)
            else:
                emit_final(sq, i)

        for sq in range(NSEQ):
            for i in range(NT):
                visit_0(sq, i)
            for l in range(DEPTH):
                for d in range(2):
                    for k in range(8):
                        P.op("pool", lambda e: e.memset(carry[:, d, k:k + 1], 0.0), writes=(carry_b[d][k],))
                build_esx(l)
                for i in range(NT - 1, -1, -1):
                    visit_A(l, i, i == NT - 1, (i - 1) if i > 0 else None)
                for i in range(NT):
                    visit_B(sq, l, i)
        P._wait("sp", list(ystore_toks.values()))
        build.ninstr = P.ninstr
    return nc


_CACHE = {}


def _get_nc(cfg_key):
    if cfg_key not in _CACHE:
        _CACHE[cfg_key] = build(Cfg(*cfg_key))
    return _CACHE[cfg_key]


def run_cores(seqs_per_core, weights, S, DEPTH):
    NSEQ = seqs_per_core[0].shape[0]
    nc = _get_nc((S, NSEQ, DEPTH))
    ident = np.eye(128, dtype=np.float32)
    oh = make_onehot()
    in_maps = []
    for c in range(len(seqs_per_core)):
        m = {n: np.ascontiguousarray(weights[n], dtype=np.float32) for n in WEIGHT_NAMES}
        m["xin"] = np.ascontiguousarray(seqs_per_core[c], dtype=np.float32)
        m["ident"] = ident
        m["onehot"] = oh
        in_maps.append(m)
    res = run_bass_kernel_spmd(nc, in_maps, core_ids=list(range(len(seqs_per_core))))
    return [r["y"] for r in res.results]


def kernel(**inputs):
    xp = np.asarray(inputs["x_prompt"], dtype=np.float32)
    xs = np.asarray(inputs["x_sample"], dtype=np.float32)
    seqs = [xp[0], xp[1]] + [xs[i] for i in range(8)]
    per_core = []
    for c in range(8):
        second = seqs[8 + c] if c < 2 else seqs[c]
        per_core.append(np.stack([seqs[c], second], axis=0))
    weights = {n: inputs[n] for n in WEIGHT_NAMES}
    ys = run_cores(per_core, weights, 8192, 4)
    out = [ys[c][0] for c in range(8)] + [ys[0][1], ys[1][1]]
    y_prompt = np.stack(out[0:2], axis=0).astype(np.float32)
    y_sample = np.stack(out[2:10], axis=0).astype(np.float32)
    return (y_prompt, y_sample)
```

```python
import math
from contextlib import ExitStack

import numpy as np

import concourse.bass as bass
import concourse.mybir as mybir
from concourse.bass_utils import run_bass_kernel_spmd

F32 = mybir.dt.float32
BF16 = mybir.dt.bfloat16
AF = mybir.ActivationFunctionType
ALU = mybir.AluOpType

D = 1024
DFF = 2816
NJ = DFF // 128
T = 512
HALO = 128
WIN_W = T + 2 * HALO
C0, C1 = HALO, HALO + T
NSLOT = 7
EPS = 1e-6
NEG = -30000.0
GELU_C = 1.5957691216057308

P_F1, P_MIX, P_F2, P_CW, P_CB, P_LAM, P_BA, P_BX, P_FIN = 0, 8, 16, 24, 56, 64, 80, 96, 112


def t5_buckets(rel):
    n = 16
    max_exact = 8
    ret = (rel > 0).astype(np.int32) * n
    na = np.abs(rel)
    large = max_exact + (np.log(np.maximum(na, 1) / max_exact) / math.log(128 / max_exact) * (n - max_exact)).astype(np.int32)
    large = np.minimum(large, n - 1)
    return ret + np.where(na < max_exact, na, large)


def make_onehot():
    oh = np.zeros((33, 1024), np.float32)
    m = np.arange(1024)
    rel = m - 512
    inw = np.abs(rel) <= 128
    b = t5_buckets(rel)
    oh[b[inw], m[inw]] = 1.0
    oh[32, ~inw] = NEG
    return oh


class Buf:
    __slots__ = ("name", "w", "r", "sem", "cnt")

    def __init__(self, name, inherit=None):
        self.name = name
        self.w = None
        self.r = dict(inherit) if inherit else {}
        self.sem = None
        self.cnt = 0


class TB:
    __slots__ = ("t", "b")

    def __init__(self, t, b):
        self.t = t
        self.b = b


class Prog:
    def __init__(self, nc, es):
        self.nc = nc
        self.es = es
        self.eng = {"pe": nc.tensor, "act": nc.scalar, "dve": nc.vector, "pool": nc.gpsimd, "sp": nc.sync}
        self.sem = {e: es.enter_context(nc.semaphore("s_" + e)) for e in ("pe", "act", "dve", "pool")}
        self.cnt = {e: 0 for e in self.sem}
        self.seen = {e: {} for e in self.eng}
        self.dsems = {}
        self.dcnt = {}
        self.inherit = {}
        self.banks = []
        self.bank_i = 0
        self.slots = []
        self.slot_i = 0
        self.ninstr = 0

    def buf(self, name, staged=False):
        return Buf(name, self.inherit if staged else None)

    def stage_end(self, bufs):
        for b in bufs:
            if b.w is not None:
                k, v = b.w
                if v > self.inherit.get(k, 0):
                    self.inherit[k] = v
            for k, v in b.r.items():
                if v > self.inherit.get(k, 0):
                    self.inherit[k] = v

    def _semobj(self, k):
        return self.sem[k] if k in self.sem else self.dsems[k]

    def _wait(self, e, toks):
        best = {}
        for t in toks:
            if t is None:
                continue
            k, v = t
            if v > best.get(k, 0):
                best[k] = v
        for k, v in best.items():
            if self.seen[e].get(k, 0) >= v:
                continue
            self.eng[e].wait_ge(self._semobj(k), v)
            self.ninstr += 1
            self.seen[e][k] = v

    def _deps(self, e, reads, writes):
        toks = []
        for b in reads:
            toks.append(b.w)
        for b in writes:
            if b.w is not None and b.w[0] != e:
                toks.append(b.w)
            for k, v in b.r.items():
                if k != e:
                    toks.append((k, v))
        return toks

    def _record(self, tok, reads, writes):
        k, v = tok
        for b in writes:
            b.w = tok
            b.r = {}
        for b in reads:
            if b in writes:
                continue
            if v > b.r.get(k, 0):
                b.r[k] = v

    def op(self, e, fn, reads=(), writes=()):
        self._wait(e, self._deps(e, reads, writes))
        ins = fn(self.eng[e])
        self.cnt[e] += 1
        ins.then_inc(self.sem[e], 1)
        self.ninstr += 1
        tok = (e, self.cnt[e])
        self._record(tok, reads, writes)
        return tok

    def bank(self):
        b = self.banks[self.bank_i % len(self.banks)]
        self.bank_i += 1
        return b

    def mm(self, bank, mms, reads=(), first=True, last=True):
        toks = [b.w for b in reads]
        if first:
            toks += self._deps("pe", (), (bank.b,))
        self._wait("pe", toks)
        n = len(mms)
        ins = None
        for i, m in enumerate(mms):
            if m[0] == "T":
                ins = self.nc.tensor.transpose(out=m[1], in_=m[2], identity=m[3])
            else:
                out_ap, lhsT, rhs, start = m
                ins = self.nc.tensor.matmul(out_ap, lhsT, rhs, start=bool(start), stop=bool(last and i == n - 1))
            self.ninstr += 1
        self.cnt["pe"] += 1
        ins.then_inc(self.sem["pe"], 1)
        tok = ("pe", self.cnt["pe"])
        self._record(tok, reads, (bank.b,) if last else ())
        if not last:
            bank.b.w = tok
            bank.b.r = {}
        return tok

    def dsem(self, buf):
        name = "d_" + buf.name
        if name not in self.dsems:
            self.dsems[name] = self.es.enter_context(self.nc.semaphore(name))
            self.dcnt[name] = 0
        return name

    def dma(self, out_ap, in_ap, sem_buf, reads=(), writes=()):
        self._wait("sp", self._deps("sp", reads, writes))
        name = self.dsem(sem_buf)
        ins = self.nc.sync.dma_start(out=out_ap, in_=in_ap)
        self.dcnt[name] += 16
        ins.then_inc(self.dsems[name], 16)
        self.ninstr += 1
        tok = (name, self.dcnt[name])
        self._record(tok, reads, writes)
        return tok

    def wload(self, dram_ap, n):
        s = self.slots[self.slot_i % len(self.slots)]
        self.slot_i += 1
        self.dma(s.t[:, 0:n], dram_ap, s.b, writes=(s.b,))
        return s


def pairview(s):
    return s.t[:, 0:2048].rearrange("p (k g c) -> p k g c", k=8, g=2)


class Cfg:
    def __init__(self, S=8192, NSEQ=2, DEPTH=4):
        self.S = S
        self.NSEQ = NSEQ
        self.DEPTH = DEPTH
        self.NT = S // T
        self.NB = S // 128
        self.SP = S + 2 * HALO


WEIGHT_NAMES = ["ffn1_norm", "ffn1_w_up", "ffn1_w_down", "mix_norm", "w_in", "conv_w", "conv_b", "rg_lambda",
                "rg_w_a", "rg_b_a", "rg_w_x", "rg_b_x", "attn_sink", "rel_bias_table", "w_br_attn", "w_br_rnn",
                "w_out", "ffn2_norm", "ffn2_w_up", "ffn2_w_down", "final_norm"]
WEIGHT_SHAPES = {
    "ffn1_norm": [4, 1024], "ffn1_w_up": [4, 1024, 5632], "ffn1_w_down": [4, 2816, 1024], "mix_norm": [4, 1024],
    "w_in": [4, 1024, 5632], "conv_w": [4, 4, 1024], "conv_b": [4, 1024], "rg_lambda": [4, 2, 1024],
    "rg_w_a": [4, 2, 8, 128, 128], "rg_b_a": [4, 2, 1024], "rg_w_x": [4, 2, 8, 128, 128], "rg_b_x": [4, 2, 1024],
    "attn_sink": [4, 8], "rel_bias_table": [32, 8], "w_br_attn": [4, 1024, 1024], "w_br_rnn": [4, 1024, 1024],
    "w_out": [4, 1024, 1024], "ffn2_norm": [4, 1024], "ffn2_w_up": [4, 1024, 5632], "ffn2_w_down": [4, 2816, 1024],
    "final_norm": [1024],
}


def build(cfg):
    nc = bass.Bass("TRN2", target_bir_lowering=False)
    S, NSEQ, DEPTH, NT, NB, SPAD = cfg.S, cfg.NSEQ, cfg.DEPTH, cfg.NT, cfg.NB, cfg.SP
    inp = {}
    for n in WEIGHT_NAMES:
        inp[n] = nc.dram_tensor(n, WEIGHT_SHAPES[n], F32, kind="ExternalInput").ap()
    xin = nc.dram_tensor("xin", [NSEQ, S, D], F32, kind="ExternalInput").ap()
    ident_d = nc.dram_tensor("ident", [128, 128], F32, kind="ExternalInput").ap()
    oh_d = nc.dram_tensor("onehot", [33, 1024], F32, kind="ExternalInput").ap()
    yout = nc.dram_tensor("y", [NSEQ, S, D], F32, kind="ExternalOutput").ap()

    xbuf = [nc.dram_tensor(f"xbuf{i}", [8, 128, SPAD], F32, kind="Internal").ap() for i in range(2)]
    hbst = nc.dram_tensor("hbst", [8, 128, S], F32, kind="Internal").ap()
    xcst = nc.dram_tensor("xcst", [8, 128, S], F32, kind="Internal").ap()
    biasF = nc.dram_tensor("biasF", [8, 1024], F32, kind="Internal").ap()
    WUP = [[nc.dram_tensor(f"wup{l}_{f}", [NJ, 128, 2048], BF16, kind="Internal").ap() for f in range(2)] for l in range(DEPTH)]
    WDN = [[nc.dram_tensor(f"wdn{l}_{f}", [16, 128, 1408], BF16, kind="Internal").ap() for f in range(2)] for l in range(DEPTH)]
    WIN = [nc.dram_tensor(f"win{l}", [22, 128, 2048], BF16, kind="Internal").ap() for l in range(DEPTH)]
    WBR = [[nc.dram_tensor(f"wbr{l}_{a}", [4, 128, 2048], BF16, kind="Internal").ap() for a in range(2)] for l in range(DEPTH)]
    WOUT = [nc.dram_tensor(f"wout{l}", [4, 128, 2048], BF16, kind="Internal").ap() for l in range(DEPTH)]
    WG = [nc.dram_tensor(f"wg{l}", [2, 128, 2048], BF16, kind="Internal").ap() for l in range(DEPTH)]

    with ExitStack() as es:
        P = Prog(nc, es)
        E = es.enter_context

        uid = [0]

        def SB(name, shape, dt=F32, stack=None):
            if stack is not None:
                uid[0] += 1
                name = f"{name}_{uid[0]}"
            return (stack or es).enter_context(nc.sbuf_tensor(name, shape, dt))

        identf = SB("identf", [128, 128]); identf_b = P.buf("identf")
        identb = SB("identb", [128, 128], BF16); identb_b = P.buf("identb")
        onesb = SB("onesb", [128, 128], BF16); onesb_b = P.buf("onesb")
        zt = SB("zt", [128, 128]); zt_b = P.buf("zt")
        prm = SB("prm", [128, DEPTH, 120]); prm_b = P.buf("prm")
        cs = SB("cs", [128, DEPTH, 16]); cs_b = P.buf("cs")
        est = SB("es", [128, 32]); es_b = P.buf("es")
        esx = SB("esx", [128, 2, 4, 128]); esx_b = P.buf("esx")
        Bhi = SB("Bhi", [128, 8, 384], BF16); Bhi_b = P.buf("Bhi")
        Blo = SB("Blo", [128, 8, 384], BF16); Blo_b = P.buf("Blo")
        carry = SB("carry", [128, 2, 8]); carry_b = [[P.buf(f"carry{d}_{k}") for k in range(8)] for d in range(2)]
        for i in range(8):
            P.banks.append(TB(E(nc.psum_tensor(f"bank{i}", [128, 512], F32)), P.buf(f"bank{i}")))

        P.dma(identf[:], ident_d[:, :], identf_b, writes=(identf_b,))
        P.op("dve", lambda e: e.tensor_copy(out=identb[:], in_=identf[:]), reads=(identf_b,), writes=(identb_b,))
        P.op("dve", lambda e: e.memset(onesb[:], 1.0), writes=(onesb_b,))
        P.op("dve", lambda e: e.memset(zt[:], 0.0), writes=(zt_b,))
        ztok = []
        for xb_ in xbuf:
            for k in range(8):
                ztok.append(P.dma(xb_[k, :, 0:HALO], zt[:], zt_b, reads=(zt_b,)))
                ztok.append(P.dma(xb_[k, :, HALO + S:SPAD], zt[:], zt_b, reads=(zt_b,)))
        P._wait("sp", ztok[-1:])

        with ExitStack() as ss:
            for l in range(DEPTH):
                stg = SB(f"stg{l}", [128, 128], stack=ss); stg_b = P.buf(f"stg{l}", staged=True)
                rows = [(P_F1, inp["ffn1_norm"][l].rearrange("(k p) -> k p", p=128), 8),
                        (P_MIX, inp["mix_norm"][l].rearrange("(k p) -> k p", p=128), 8),
                        (P_F2, inp["ffn2_norm"][l].rearrange("(k p) -> k p", p=128), 8),
                        (P_CW, inp["conv_w"][l].rearrange("t (k p) -> (t k) p", p=128), 32),
                        (P_CB, inp["conv_b"][l].rearrange("(k p) -> k p", p=128), 8),
                        (P_LAM, inp["rg_lambda"][l].rearrange("d (k p) -> (d k) p", p=128), 16),
                        (P_BA, inp["rg_b_a"][l].rearrange("d (k p) -> (d k) p", p=128), 16),
                        (P_BX, inp["rg_b_x"][l].rearrange("d (k p) -> (d k) p", p=128), 16),
                        (P_FIN, inp["final_norm"].rearrange("(k p) -> k p", p=128), 8)]
                for (r0, ap, n) in rows:
                    P.dma(stg[r0:r0 + n, :], ap, stg_b, writes=(stg_b,))
                bk = P.bank()
                P.mm(bk, [("T", bk.t[:, 0:120], stg[0:120, :], identf[0:120, 0:120])], reads=(stg_b, identf_b))
                P.op("dve", lambda e: e.tensor_copy(out=prm[:, l, :], in_=bk.t[:, 0:120]), reads=(bk.b,), writes=(prm_b,))
                P.stage_end([stg_b])
            for l in range(DEPTH):
                P.op("act", lambda e: e.activation(out=cs[:, l, :], in_=prm[:, l, P_LAM:P_LAM + 16], func=AF.Exp, scale=-1.0),
                     reads=(prm_b,), writes=(cs_b,))
            for l in range(DEPTH):
                P.op("act", lambda e: e.activation(out=cs[:, l, :], in_=cs[:, l, :], func=AF.Ln, bias=1.0),
                     reads=(cs_b,), writes=(cs_b,))
            P.op("dve", lambda e: e.tensor_scalar(out=cs[:], in0=cs[:], scalar1=-8.0, scalar2=None, op0=ALU.mult),
                 reads=(cs_b,), writes=(cs_b,))
            P.dma(est[:], bass.AP(inp["attn_sink"].tensor, 0, [[0, 128], [1, 32]]), es_b, writes=(es_b,))
            P.op("act", lambda e: e.activation(out=est[:], in_=est[:], func=AF.Exp), reads=(es_b,), writes=(es_b,))

            tab = SB("tab", [33, 8], stack=ss); tab_b = P.buf("tab", staged=True)
            ohs = SB("ohs", [33, 1024], stack=ss); ohs_b = P.buf("ohs", staged=True)
            Fs = SB("Fs", [8, 1024], stack=ss); Fs_b = P.buf("Fs", staged=True)
            Tr = SB("Tr", [128, 8, 384], stack=ss); Tr_b = P.buf("Tr", staged=True)
            Bf = SB("Bf", [128, 8, 384], stack=ss); Bf_b = P.buf("Bf", staged=True)
            P.op("dve", lambda e: e.memset(tab[:], 1.0), writes=(tab_b,))
            P.dma(tab[0:32, :], inp["rel_bias_table"][:, :], tab_b, reads=(), writes=(tab_b,))
            P.dma(ohs[:], oh_d[:, :], ohs_b, writes=(ohs_b,))
            for hh in range(2):
                bk = P.bank()
                P.mm(bk, [(bk.t[0:8, :], tab[0:33, 0:8], ohs[0:33, hh * 512:(hh + 1) * 512], True)], reads=(tab_b, ohs_b))
                P.op("dve", lambda e: e.tensor_copy(out=Fs[:, hh * 512:(hh + 1) * 512], in_=bk.t[0:8, :]), reads=(bk.b,), writes=(Fs_b,))
            tk = P.dma(biasF[:, :], Fs[:], Fs_b, reads=(Fs_b,))
            P._wait("sp", [tk])
            for h in range(8):
                P.dma(Tr[:, h, :].rearrange("p (o c) -> p o c", o=3),
                      bass.AP(biasF.tensor, h * 1024 + 257, [[1, 128], [128, 3], [1, 128]]), Tr_b, writes=(Tr_b,))
            for h in range(8):
                for o in range(3):
                    P.op("dve", lambda e: e.tensor_copy(out=Bf[:, h, o * 128:(o + 1) * 128], in_=Tr[:, h, o * 128:(o + 1) * 128][:, ::-1]),
                         reads=(Tr_b,), writes=(Bf_b,))
            P.op("dve", lambda e: e.tensor_copy(out=Bhi[:], in_=Bf[:]), reads=(Bf_b,), writes=(Bhi_b,))
            P.op("dve", lambda e: e.tensor_tensor(out=Bf[:], in0=Bf[:], in1=Bhi[:], op=ALU.subtract), reads=(Bf_b, Bhi_b), writes=(Bf_b,))
            P.op("dve", lambda e: e.tensor_copy(out=Blo[:], in_=Bf[:]), reads=(Bf_b,), writes=(Blo_b,))
            P.stage_end([tab_b, ohs_b, Fs_b, Tr_b, Bf_b])

        with ExitStack() as ss:
            NCB = 2
            CW = 11264
            cf = [TB(SB(f"cf{i}", [128, CW], stack=ss), P.buf(f"cf{i}", staged=True)) for i in range(NCB)]
            cb = [TB(SB(f"cb{i}", [128, CW], BF16, stack=ss), P.buf(f"cb{i}", staged=True)) for i in range(NCB)]
            cast_i = [0]
            last_store = {}
            eng_rr = [0]

            def job(loads, n, ngrp, in_view, out_view, stores):
                i = cast_i[0]
                cast_i[0] += 1
                A = cf[i % NCB]
                Bt = cb[i % NCB]
                for vf, dap in loads:
                    P.dma(vf(A.t), dap, A.b, writes=(A.b,))
                iv = in_view(A.t)
                ov = out_view(Bt.t)
                for gi in range(ngrp):
                    e_ = ("dve", "act", "dve", "pool")[eng_rr[0] % 4]
                    eng_rr[0] += 1
                    if e_ == "act":
                        P.op("act", lambda e: e.activation(out=ov(gi), in_=iv(gi), func=AF.Copy), reads=(A.b,), writes=(Bt.b,))
                    else:
                        P.op(e_, lambda e: e.tensor_copy(out=ov(gi), in_=iv(gi)), reads=(A.b,), writes=(Bt.b,))
                for dap, vf in stores:
                    last_store[Bt.b.name] = P.dma(dap, vf(Bt.t), Bt.b, reads=(Bt.b,))

            for l in range(DEPTH):
                for f, nm in enumerate(("ffn1_w_up", "ffn2_w_up")):
                    wt = inp[nm].tensor
                    base = l * 1024 * 5632
                    for kg in range(4):
                        loads = []
                        for g in range(2):
                            src = bass.AP(wt, base + (kg * 2) * 128 * 5632 + g * DFF, [[5632, 128], [128 * 5632, 2], [1, DFF]])
                            loads.append(((lambda t, g=g: t[:, 0:CW].rearrange("p (kk g c) -> p kk g c", kk=2, g=2)[:, :, g, :]), src))
                        in_view = lambda t: (lambda gi: t[:, gi * DFF:(gi + 1) * DFF].rearrange("p (j c) -> p j c", c=128))
                        out_view = lambda t: (lambda gi: t[:, 0:CW].rearrange("p (j q c) -> p j q c", j=NJ, q=4)[:, :, gi, :])
                        dst = bass.AP(WUP[l][f].tensor, kg * 512, [[2048, 128], [128 * 2048, NJ], [1, 512]])
                        job(loads, CW, 4, in_view, out_view, [(dst, lambda t: t[:, 0:CW].rearrange("p (j r) -> p j r", j=NJ))])
                wt = inp["w_in"].tensor
                base = l * 1024 * 5632
                for kg in range(2):
                    for hh in range(2):
                        src = bass.AP(wt, base + (kg * 4) * 128 * 5632 + hh * 2816, [[5632, 128], [128 * 5632, 4], [1, 2816]])
                        loads = [((lambda t: t[:, 0:CW].rearrange("p (kk c) -> p kk c", kk=4)), src)]
                        in_view = lambda t: (lambda gi: t[:, gi * 2816:(gi + 1) * 2816].rearrange("p (u c) -> p u c", c=256))
                        out_view = lambda t: (lambda gi: t[:, 0:CW].rearrange("p (u q c) -> p u q c", u=11, q=4)[:, :, gi, :])
                        dst = bass.AP(WIN[l].tensor, hh * 11 * 128 * 2048 + kg * 1024, [[2048, 128], [128 * 2048, 11], [1, 1024]])
                        job(loads, CW, 4, in_view, out_view, [(dst, lambda t: t[:, 0:CW].rearrange("p (u r) -> p u r", u=11))])
                for f, nm in enumerate(("ffn1_w_down", "ffn2_w_down")):
                    wt = inp[nm].tensor
                    base = l * DFF * 1024
                    for jh in range(2):
                        src = bass.AP(wt, base + jh * 11 * 128 * 1024, [[1024, 128], [128 * 1024, 11], [1, 1024]])
                        loads = [((lambda t: t[:, 0:CW].rearrange("p (jj c) -> p jj c", jj=11)), src)]
                        in_view = lambda t: (lambda gi: t[:, gi * 1024:(gi + 1) * 1024].rearrange("p (m c) -> p m c", c=128))
                        out_view = lambda t: (lambda gi: t[:, 0:CW].rearrange("p (m jj c) -> p m jj c", m=8, jj=11)[:, :, gi, :])
                        dst = bass.AP(WDN[l][f].tensor, jh * 128 * 1408, [[1408, 128], [2 * 128 * 1408, 8], [1, 1408]])
                        job(loads, CW, 11, in_view, out_view, [(dst, lambda t: t[:, 0:CW].rearrange("p (m r) -> p m r", m=8))])
                for nm, dstT in (("w_br_attn", WBR[l][0]), ("w_br_rnn", WBR[l][1]), ("w_out", WOUT[l])):
                    wt = inp[nm].tensor
                    base = l * 1024 * 1024
                    src = bass.AP(wt, base, [[1024, 128], [128 * 1024, 8], [1, 1024]])
                    loads = [((lambda t: t[:, 0:8192].rearrange("p (kk c) -> p kk c", kk=8)), src)]
                    in_view = lambda t: (lambda gi: t[:, gi * 1024:(gi + 1) * 1024].rearrange("p (u c) -> p u c", c=256))
                    out_view = lambda t: (lambda gi: t[:, 0:8192].rearrange("p (u q c) -> p u q c", u=4, q=8)[:, :, gi, :])
                    dst = bass.AP(dstT.tensor, 0, [[2048, 128], [128 * 2048, 4], [1, 2048]])
                    job(loads, 8192, 8, in_view, out_view, [(dst, lambda t: t[:, 0:8192].rearrange("p (u r) -> p u r", u=4))])
                loads = []
                for gi_, nm in enumerate(("rg_w_a", "rg_w_x")):
                    src = bass.AP(inp[nm].tensor, l * 2 * 8 * 128 * 128, [[128, 128], [128 * 128, 16], [1, 128]])
                    loads.append(((lambda t, gi_=gi_: t[:, gi_ * 2048:(gi_ + 1) * 2048].rearrange("p (b c) -> p b c", c=128)), src))
                in_view = lambda t: (lambda gi: t[:, gi * 2048:(gi + 1) * 2048].rearrange("p (b c) -> p b c", c=128))
                out_view = lambda t: (lambda gi: t[:, 0:4096].rearrange("p (b q c) -> p b q c", b=16, q=2)[:, :, gi, :])
                dst = bass.AP(WG[l].tensor, 0, [[2048, 128], [128 * 2048, 2], [1, 2048]])
                job(loads, 4096, 2, in_view, out_view, [(dst, lambda t: t[:, 0:4096].rearrange("p (d r) -> p d r", d=2))])
            P._wait("sp", list(last_store.values()))
            P.stage_end([x.b for x in cf] + [x.b for x in cb])

        xW = SB("xW", [128, 8, WIN_W]); xw = [P.buf(f"xw{k}") for k in range(8)]
        xR = SB("xR", [128, 8, T]); xr_ = [P.buf(f"xr_{k}") for k in range(8)]
        hT = SB("hT", [128, 8, WIN_W], BF16); hT_b = P.buf("hT")
        rstd = SB("rstd", [128, WIN_W]); rstd_b = P.buf("rstd")
        for i in range(NSLOT):
            P.slots.append(TB(SB(f"slot{i}", [128, 2048], BF16), P.buf(f"slot{i}")))
        attnT = SB("attnT", [128, 8, T], BF16); attn_b = [P.buf(f"attn{g}") for g in range(2)]
        gy = SB("gy", [128, 8, T], BF16); gy_b = [P.buf(f"gy{k}") for k in range(8)]
        mrg = SB("mrg", [128, 8, T], BF16); mrg_b = [P.buf(f"mrg{k}") for k in range(8)]
        mtmp = [TB(SB(f"mtmp{i}", [128, T]), P.buf(f"mtmp{i}")) for i in range(4)]
        xcB = SB("xcB", [128, 8, T]); xcB_b = [P.buf(f"xcB{k}") for k in range(8)]
        xld_b = P.buf("xld")
        xcs_b = P.buf("xcs")
        xcl_b = P.buf("xcl")
        xst_b = P.buf("xst")

        xreg = [[P.buf(f"xreg{i}_{t}") for t in range(NT)] for i in range(2)]
        hbreg = [[P.buf(f"hbreg{t}_{k}") for k in range(8)] for t in range(NT)]
        xcreg = [P.buf(f"xcreg{t}") for t in range(NT)]

        def load_window(pp, i):
            t0 = i * T
            regs = [xreg[pp][j] for j in (i - 1, i, i + 1) if 0 <= j < NT]
            src = bass.AP(xbuf[pp].tensor, t0, [[SPAD, 128], [128 * SPAD, 8], [1, WIN_W]])
            P.dma(xW[:, :, :], src, xld_b, reads=regs, writes=xw)

        def store_center(pp, i):
            t0 = i * T
            dst = bass.AP(xbuf[pp].tensor, HALO + t0, [[SPAD, 128], [128 * SPAD, 8], [1, T]])
            P.dma(dst, xR[:, :, :], xst_b, reads=xr_, writes=(xreg[pp][i],))

        def emit_norm(c0, c1, gcol, out_t, out_b, win=True):
            st_, sb_, off = (xW, xw, 0) if win else (xR, xr_, C0)
            for k in range(8):
                P.op("act", lambda e: e.activation(out=hT[:, k, c0:c1], in_=st_[:, k, c0 - off:c1 - off], func=AF.Square),
                     reads=(sb_[k],), writes=(hT_b,))
            for a in range(c0, c1, 512):
                b = min(a + 512, c1)
                bk = P.bank()
                P.mm(bk, [(bk.t[:, 0:b - a], onesb[:], hT[:, k, a:b], k == 0) for k in range(8)], reads=(hT_b, onesb_b))
                P.op("act", lambda e: e.activation(out=rstd[:, a:b], in_=bk.t[:, 0:b - a], func=AF.Sqrt, scale=1.0 / D, bias=EPS),
                     reads=(bk.b,), writes=(rstd_b,))
            P.op("dve", lambda e: e.reciprocal(out=rstd[:, c0:c1], in_=rstd[:, c0:c1]), reads=(rstd_b,), writes=(rstd_b,))
            for k in range(8):
                oap = out_t[:, k, c0:c1] if out_t is hT else out_t[:, k, c0 - C0:c1 - C0]
                P.op("dve", lambda e: e.scalar_tensor_tensor(out=oap, in0=st_[:, k, c0 - off:c1 - off], scalar=gcol(k),
                                                             in1=rstd[:, c0:c1], op0=ALU.mult, op1=ALU.mult),
                     reads=(sb_[k], rstd_b, prm_b), writes=(out_b,))

        def emit_ffn(l, f, hook=None):
            pc = P_F1 if f == 0 else P_F2
            gcol = lambda k: prm[:, l, pc + k:pc + k + 1]
            with ExitStack() as ss:
                act = SB("act", [128, NJ, T], BF16, stack=ss); act_b = [P.buf(f"act{j}", staged=True) for j in range(NJ)]
                silu = [TB(SB(f"silu{i}", [128, T], stack=ss), P.buf(f"silu{i}", staged=True)) for i in range(2)]
                rsb = [TB(SB(f"rsb{i}", [128, T], stack=ss), P.buf(f"rsb{i}", staged=True)) for i in range(2)]
                sqs = SB("sqs", [128, 8, T], BF16, stack=ss); sqs_b = P.buf("sqs", staged=True)
                for k in range(8):
                    e_ = "pool" if k in (2, 5, 7) else "dve"
                    P.op(e_, lambda e: e.tensor_scalar(out=hT[:, k, C0:C1], in0=xR[:, k, :], scalar1=gcol(k), scalar2=1.0, op0=ALU.mult, op1=ALU.mult),
                         reads=(xr_[k], prm_b), writes=(hT_b,))
                for k in range(8):
                    P.op("act", lambda e: e.activation(out=sqs[:, k, :], in_=xR[:, k, :], func=AF.Square), reads=(xr_[k],), writes=(sqs_b,))
                for j in range(NJ):
                    s = P.wload(WUP[l][f][j], 2048)
                    if j == 3 and hook is not None:
                        hook()
                    wv = pairview(s)
                    bg = P.bank(); bu = P.bank()
                    P.mm(bg, [(bg.t[:, :], wv[:, k, 0, :], hT[:, k, C0:C1], k == 0) for k in range(8)], reads=(hT_b, s.b))
                    P.mm(bu, [(bu.t[:, :], wv[:, k, 1, :], hT[:, k, C0:C1], k == 0) for k in range(8)], reads=(hT_b, s.b))
                    if j == 0:
                        bk = P.bank()
                        P.mm(bk, [(bk.t[:, :], onesb[:], sqs[:, k, :], k == 0) for k in range(8)], reads=(sqs_b, onesb_b))
                        P.op("act", lambda e: e.activation(out=rstd[:, C0:C1], in_=bk.t[:, :], func=AF.Sqrt, scale=1.0 / D, bias=EPS),
                             reads=(bk.b,), writes=(rstd_b,))
                        P.op("dve", lambda e: e.reciprocal(out=rstd[:, C0:C1], in_=rstd[:, C0:C1]), reads=(rstd_b,), writes=(rstd_b,))
                    tm_ = silu[j % 2]
                    rs_ = rsb[j % 2]
                    P.op("dve", lambda e: e.tensor_tensor(out=tm_.t[:], in0=bg.t[:, :], in1=rstd[:, C0:C1], op=ALU.mult),
                         reads=(bg.b, rstd_b), writes=(tm_.b,))
                    P.op("act", lambda e: e.activation(out=tm_.t[:], in_=tm_.t[:], func=AF.Silu), reads=(tm_.b,), writes=(tm_.b,))
                    P.op("pool", lambda e: e.tensor_tensor(out=rs_.t[:], in0=tm_.t[:], in1=rstd[:, C0:C1], op=ALU.mult),
                         reads=(tm_.b, rstd_b), writes=(rs_.b,))
                    P.op("dve", lambda e: e.tensor_tensor(out=act[:, j, :], in0=bu.t[:, :], in1=rs_.t[:], op=ALU.mult),
                         reads=(bu.b, rs_.b), writes=(act_b[j],))
                bos = [P.bank() for _ in range(8)]
                for jh in range(2):
                    for m in range(8):
                        bo = bos[m]
                        s = P.wload(WDN[l][f][m * 2 + jh], 1408)
                        wv = s.t[:, 0:1408].rearrange("p (j c) -> p j c", j=11)
                        P.mm(bo, [(bo.t[:, :], wv[:, jj, :], act[:, jh * 11 + jj, :], (jh == 0 and jj == 0)) for jj in range(11)],
                             reads=tuple(act_b[jh * 11:(jh + 1) * 11]) + (s.b,), first=(jh == 0), last=(jh == 1))
                        if jh == 1:
                            P.op("dve", lambda e: e.scalar_tensor_tensor(out=xR[:, m, :], in0=bo.t[:, :], scalar=0.5, in1=xR[:, m, :],
                                                                         op0=ALU.mult, op1=ALU.add),
                                 reads=(bo.b, xr_[m]), writes=(xr_[m],))
                P.stage_end(act_b + [x.b for x in silu] + [x.b for x in rsb] + [sqs_b])

        def emit_xr_conv(l, ss, hook=None, stats=None):
            xr4 = SB("xr", [128, 4, 516], stack=ss); xr_b4 = [P.buf(f"xr{k}", staged=True) for k in range(4)]

            class _XR:
                def __getitem__(self, key):
                    p_, kk_, c_ = key
                    return xr4[p_, ((kk_ // 2) % 2) * 2 + (kk_ % 2), c_]
            xr = _XR()
            xr_b = [xr_b4[((k // 2) % 2) * 2 + (k % 2)] for k in range(8)]
            xc = xcB; xc_b = xcB_b
            xcbf = SB("xcbf", [128, 8, T], BF16, stack=ss); xcbf_b = [P.buf(f"xcbf{k}", staged=True) for k in range(8)]
            slots_ = [P.wload(WIN[l][6 + u], 2048) for u in range(4)]
            if hook is not None:
                hook()
            for u in range(4):
                s = slots_[u]
                wv = pairview(s)
                for g in range(2):
                    kk = 2 * u + g
                    bm = P.bank(); be = P.bank()
                    P.mm(bm, [(bm.t[:, :], wv[:, k, g, :], hT[:, k, C0:C1], k == 0) for k in range(8)], reads=(hT_b, s.b))
                    edges = []
                    for k in range(8):
                        edges.append((be.t[:, 0:2], wv[:, k, g, :], hT[:, k, C0 - 2:C0], k == 0))
                        edges.append((be.t[:, 2:3], wv[:, k, g, :], hT[:, k, C1:C1 + 1], False))
                    P.mm(be, edges, reads=(hT_b, s.b))
                    if stats is not None:
                        if kk == 0:
                            stats()
                        P.op("dve", lambda e: e.tensor_tensor(out=xr[:, kk, 2:514], in0=bm.t[:, :], in1=rstd[:, C0:C1], op=ALU.mult),
                             reads=(bm.b, rstd_b), writes=(xr_b[kk],))
                        P.op("dve", lambda e: e.tensor_tensor(out=xr[:, kk, 0:2], in0=be.t[:, 0:2], in1=rstd[:, C0 - 2:C0], op=ALU.mult),
                             reads=(be.b, rstd_b), writes=(xr_b[kk],))
                        P.op("dve", lambda e: e.tensor_tensor(out=xr[:, kk, 514:515], in0=be.t[:, 2:3], in1=rstd[:, C1:C1 + 1], op=ALU.mult),
                             reads=(be.b, rstd_b), writes=(xr_b[kk],))
                        continue
                    P.op("act", lambda e: e.activation(out=xr[:, kk, 2:514], in_=bm.t[:, :], func=AF.Copy), reads=(bm.b,), writes=(xr_b[kk],))
                    P.op("act", lambda e: e.activation(out=xr[:, kk, 0:2], in_=be.t[:, 0:2], func=AF.Copy), reads=(be.b,), writes=(xr_b[kk],))
                    P.op("act", lambda e: e.activation(out=xr[:, kk, 514:515], in_=be.t[:, 2:3], func=AF.Copy), reads=(be.b,), writes=(xr_b[kk],))
                cw = lambda tap, kk: prm[:, l, P_CW + tap * 8 + kk:P_CW + tap * 8 + kk + 1]
                for g in range(2):
                    kk = 2 * u + g
                    P.op("pool", lambda e: e.tensor_scalar(out=xc[:, kk, :], in0=xr[:, kk, 0:T], scalar1=cw(0, kk),
                                                           scalar2=prm[:, l, P_CB + kk:P_CB + kk + 1], op0=ALU.mult, op1=ALU.add),
                         reads=(xr_b[kk], prm_b), writes=(xc_b[kk],))
                for tap in range(1, 4):
                    for g in range(2):
                        kk = 2 * u + g
                        P.op("dve", lambda e: e.scalar_tensor_tensor(out=xc[:, kk, :], in0=xr[:, kk, tap:tap + T], scalar=cw(tap, kk),
                                                                     in1=xc[:, kk, :], op0=ALU.mult, op1=ALU.add),
                             reads=(xr_b[kk], xc_b[kk], prm_b), writes=(xc_b[kk],))
                for g in range(2):
                    kk = 2 * u + g
                    P.op("dve", lambda e: e.tensor_copy(out=xcbf[:, kk, :], in_=xc[:, kk, :]), reads=(xc_b[kk],), writes=(xcbf_b[kk],))
            return dict(xr_b=xr_b4, xc=xc, xc_b=xc_b, xcbf=xcbf, xcbf_b=xcbf_b)

        def emit_chain_batch(l, d, kks, X, gslot, Rs, Hfor, post):
            gv = gslot.t[:, 0:2048].rearrange("p (b g c) -> p b g c", b=8, g=2)
            for j, kk in enumerate(kks):
                R = Rs[j]
                col = d * 8 + kk
                ba = P.bank(); bx = P.bank()
                P.mm(ba, [(ba.t[:, :], gv[:, kk, 0, :], X["xcbf"][:, kk, :], True)], reads=(X["xcbf_b"][kk], gslot.b))
                P.mm(bx, [(bx.t[:, :], gv[:, kk, 1, :], X["xcbf"][:, kk, :], True)], reads=(X["xcbf_b"][kk], gslot.b))
                P.op("act", lambda e: e.activation(out=R["r"].t[:], in_=ba.t[:, :], func=AF.Sigmoid, bias=prm[:, l, P_BA + col:P_BA + col + 1]),
                     reads=(ba.b, prm_b), writes=(R["r"].b,))
                P.op("act", lambda e: e.activation(out=R["i"].t[:], in_=bx.t[:, :], func=AF.Sigmoid, bias=prm[:, l, P_BX + col:P_BX + col + 1]),
                     reads=(bx.b, prm_b), writes=(R["i"].b,))
            for j, kk in enumerate(kks):
                R = Rs[j]
                col = d * 8 + kk
                P.op("act", lambda e: e.activation(out=R["r"].t[:], in_=R["r"].t[:], func=AF.Exp, scale=cs[:, l, col:col + 1]),
                     reads=(R["r"].b, cs_b), writes=(R["r"].b,))
                P.op("pool", lambda e: e.tensor_tensor(out=R["i"].t[:], in0=R["i"].t[:], in1=X["xc"][:, kk, :], op=ALU.mult),
                     reads=(R["i"].b, X["xc_b"][kk]), writes=(R["i"].b,))
            for j, kk in enumerate(kks):
                R = Rs[j]
                P.op("act", lambda e: e.activation(out=R["s"].t[:], in_=R["r"].t[:], func=AF.Square),
                     reads=(R["r"].b,), writes=(R["s"].b,))
            for j, kk in enumerate(kks):
                R = Rs[j]
                P.op("act", lambda e: e.activation(out=R["s"].t[:], in_=R["s"].t[:], func=AF.Sqrt, scale=-1.0, bias=1.0),
                     reads=(R["s"].b,), writes=(R["s"].b,))
            for j, kk in enumerate(kks):
                R = Rs[j]
                P.op("pool", lambda e: e.tensor_tensor(out=R["i"].t[:], in0=R["i"].t[:], in1=R["s"].t[:], op=ALU.mult),
                     reads=(R["i"].b, R["s"].b), writes=(R["i"].b,))
            for j, kk in enumerate(kks):
                R = Rs[j]
                Hh = Hfor(kk)
                cb_ = carry_b[d][kk]
                if d == 0:
                    P.op("dve", lambda e: e.tensor_tensor_scan(out=Hh.t[:], data0=R["r"].t[:], data1=R["i"].t[:], initial=carry[:, d, kk:kk + 1],
                                                               op0=ALU.mult, op1=ALU.add),
                         reads=(R["r"].b, R["i"].b, cb_), writes=(Hh.b,))
                    P.op("dve", lambda e: e.tensor_copy(out=carry[:, d, kk:kk + 1], in_=Hh.t[:, T - 1:T]), reads=(Hh.b,), writes=(cb_,))
                else:
                    P.op("dve", lambda e: e.tensor_tensor_scan(out=Hh.t[:, ::-1], data0=R["r"].t[:, ::-1], data1=R["i"].t[:, ::-1],
                                                               initial=carry[:, d, kk:kk + 1], op0=ALU.mult, op1=ALU.add),
                         reads=(R["r"].b, R["i"].b, cb_), writes=(Hh.b,))
                    P.op("dve", lambda e: e.tensor_copy(out=carry[:, d, kk:kk + 1], in_=Hh.t[:, 0:1]), reads=(Hh.b,), writes=(cb_,))
                post(kk, Hh)

        def rnn_tmps(ss, nR, nH):
            Rs = [dict(r=TB(SB(f"rr{i}", [128, T], stack=ss), P.buf(f"rr{i}", staged=True)),
                       i=TB(SB(f"ri{i}", [128, T], stack=ss), P.buf(f"ri{i}", staged=True)),
                       s=TB(SB(f"rs{i}", [128, T], stack=ss), P.buf(f"rs{i}", staged=True))) for i in range(nR)]
            Hs = [TB(SB(f"H{i}", [128, T], stack=ss), P.buf(f"H{i}", staged=True)) for i in range(nH)]
            return Rs, Hs

        def visit_A(l, i, first, nxt):
            t0 = i * T
            pp = l % 2
            if first:
                load_window(pp, i)
            a0, a1 = C0 - 2, C1 + 1
            DEFER_A = False
            if not DEFER_A:
                emit_norm(a0, a1, lambda k: prm[:, l, P_MIX + k:P_MIX + k + 1], hT, hT_b)
            for k in range(8 if DEFER_A else 0):
                e_ = "pool" if k in (2, 5, 7) else "dve"
                P.op(e_, lambda e: e.tensor_scalar(out=hT[:, k, a0:a1], in0=xW[:, k, a0:a1], scalar1=prm[:, l, P_MIX + k:P_MIX + k + 1],
                                                   scalar2=1.0, op0=ALU.mult, op1=ALU.mult),
                     reads=(xw[k], prm_b), writes=(hT_b,))
            for k in range(8 if DEFER_A else 0):
                P.op("act", lambda e: e.activation(out=mrg[:, k, :], in_=xW[:, k, a0:a0 + 512], func=AF.Square), reads=(xw[k],), writes=(mrg_b[k],))
                P.op("act", lambda e: e.activation(out=attnT[:, k, 0:3], in_=xW[:, k, a0 + 512:a1], func=AF.Square), reads=(xw[k],), writes=(attn_b[k // 4],))

            def stats():
                bk = P.bank()
                P.mm(bk, [(bk.t[:, :], onesb[:], mrg[:, k, :], k == 0) for k in range(8)], reads=tuple(mrg_b) + (onesb_b,))
                P.op("act", lambda e: e.activation(out=rstd[:, a0:a0 + 512], in_=bk.t[:, :], func=AF.Sqrt, scale=1.0 / D, bias=EPS),
                     reads=(bk.b,), writes=(rstd_b,))
                bk2 = P.bank()
                P.mm(bk2, [(bk2.t[:, 0:3], onesb[:], attnT[:, k, 0:3], k == 0) for k in range(8)], reads=(attn_b[0], attn_b[1], onesb_b))
                P.op("act", lambda e: e.activation(out=rstd[:, a0 + 512:a1], in_=bk2.t[:, 0:3], func=AF.Sqrt, scale=1.0 / D, bias=EPS),
                     reads=(bk2.b,), writes=(rstd_b,))
                P.op("dve", lambda e: e.reciprocal(out=rstd[:, a0:a1], in_=rstd[:, a0:a1]), reads=(rstd_b,), writes=(rstd_b,))
            with ExitStack() as ss:
                gs = [None]

                def hook():
                    gs[0] = P.wload(WG[l][1], 2048)
                    if nxt is not None:
                        load_window(pp, nxt)
                X = emit_xr_conv(l, ss, hook, stats if DEFER_A else None)
                P.dma(bass.AP(xcst.tensor, t0, [[S, 128], [128 * S, 8], [1, T]]), X["xc"][:, :, :], xcs_b, reads=X["xc_b"], writes=(xcreg[i],))
                Rs, Hs = rnn_tmps(ss, 4, 4)

                def post(kk, Hh):
                    P.dma(hbst[kk, :, t0:t0 + T], Hh.t[:], Hh.b, reads=(Hh.b,), writes=(hbreg[i][kk],))
                for bi in range(2):
                    emit_chain_batch(l, 1, list(range(bi * 4, bi * 4 + 4)), X, gs[0], Rs, lambda kk: Hs[kk % 4], post)
                allb = X["xr_b"] + X["xc_b"] + X["xcbf_b"] + [h.b for h in Hs]
                for R in Rs:
                    allb += [R["r"].b, R["i"].b, R["s"].b]
                P.stage_end(allb)

        def emit_attention(l, i):
            def valid(wb):
                ab = i * 4 + wb - 1
                return 0 <= ab < NB
            with ExitStack() as ss:
                qT = SB("qT", [128, 8, T], BF16, stack=ss); qT_b = [P.buf(f"q{c}", staged=True) for c in range(8)]
                kT = SB("kT", [128, 2, WIN_W], BF16, stack=ss); kT_b = P.buf("kT", staged=True)
                Vt = SB("Vt", [128, 6, 256], BF16, stack=ss); Vt_b = P.buf("Vt", staged=True)
                PT = [TB(SB(f"PT{j}", [128, 8, 384], BF16, stack=ss), P.buf(f"PT{j}", staged=True)) for j in range(2)]
                rcp = [TB(SB(f"rcp{j}", [128, T], stack=ss), P.buf(f"rcp{j}", staged=True)) for j in range(2)]
                s = P.wload(WIN[l][4], 2048); wv = pairview(s)
                for g in range(2):
                    for (a, b) in ((0, 512), (512, WIN_W)):
                        bk = P.bank()
                        P.mm(bk, [(bk.t[:, 0:b - a], wv[:, k, g, :], hT[:, k, a:b], k == 0) for k in range(8)], reads=(hT_b, s.b))
                        P.op("dve", lambda e: e.tensor_copy(out=kT[:, g, a:b], in_=bk.t[:, 0:b - a]), reads=(bk.b,), writes=(kT_b,))
                s = P.wload(WIN[l][5], 2048); wv = pairview(s)
                for wb in range(6):
                    if not valid(wb):
                        continue
                    bk = P.bank()
                    P.mm(bk, [(bk.t[:, 0:256], hT[:, k, wb * 128:(wb + 1) * 128], wv[:, k, :, :], k == 0) for k in range(8)], reads=(hT_b, s.b))
                    P.op("act", lambda e: e.activation(out=Vt[:, wb, :], in_=bk.t[:, 0:256], func=AF.Copy), reads=(bk.b,), writes=(Vt_b,))
                for u in range(4):
                    s = P.wload(WIN[l][u], 2048); wv = pairview(s)
                    for g in range(2):
                        c = 2 * u + g
                        bk = P.bank()
                        P.mm(bk, [(bk.t[:, :], wv[:, k, g, :], hT[:, k, C0:C1], k == 0) for k in range(8)], reads=(hT_b, s.b))
                        P.op("act", lambda e: e.activation(out=qT[:, c, :], in_=bk.t[:, :], func=AF.Copy, scale=128.0 ** -0.5),
                             reads=(bk.b,), writes=(qT_b[c],))

                def scores(qb):
                    slot = PT[qb % 2]
                    ois = [oi for oi in range(3) if valid(qb + oi)]
                    o0, o1 = ois[0], ois[-1] + 1
                    for h in range(8):
                        g = h // 4
                        bk = P.bank()
                        mms = []
                        for n_, oi in enumerate(ois):
                            wb = qb + oi
                            mms.append((bk.t[:, oi * 128:(oi + 1) * 128], kT[:, g, wb * 128:(wb + 1) * 128], qT[:, h, qb * 128:(qb + 1) * 128], n_ == 0))
                        mms.append((bk.t[:, o0 * 128:o1 * 128], identb[:], Bhi[:, h, o0 * 128:o1 * 128], False))
                        mms.append((bk.t[:, o0 * 128:o1 * 128], identb[:], Blo[:, h, o0 * 128:o1 * 128], False))
                        P.mm(bk, mms, reads=(kT_b, qT_b[h], identb_b, Bhi_b, Blo_b))
                        P.op("act", lambda e: e.activation(out=slot.t[:, h, o0 * 128:o1 * 128], in_=bk.t[:, o0 * 128:o1 * 128], func=AF.Exp),
                             reads=(bk.b,), writes=(slot.b,))

                def pv(qb):
                    slot = PT[qb % 2]
                    ois = [oi for oi in range(3) if valid(qb + oi)]
                    for g in range(2):
                        bo = P.bank(); bd = P.bank()
                        rhs = lambda oi: slot.t[:, 4 * g:4 * g + 4, oi * 128:(oi + 1) * 128]
                        P.mm(bo, [(bo.t[:, :], Vt[:, qb + oi, g * 128:(g + 1) * 128], rhs(oi), n_ == 0) for n_, oi in enumerate(ois)],
                             reads=(Vt_b, slot.b))
                        P.mm(bd, [(bd.t[:, :], onesb[:], rhs(oi), n_ == 0) for n_, oi in enumerate(ois)], reads=(onesb_b, slot.b))
                        rc = rcp[g]
                        P.op("dve", lambda e: e.tensor_tensor(out=rc.t[:], in0=bd.t[:, :], in1=esx[:, g, :, :].rearrange("p h c -> p (h c)"), op=ALU.add),
                             reads=(bd.b, esx_b), writes=(rc.b,))
                        P.op("dve", lambda e: e.reciprocal(out=rc.t[:], in_=rc.t[:]), reads=(rc.b,), writes=(rc.b,))
                        P.op("dve", lambda e: e.tensor_tensor(out=attnT[:, 4 * g:4 * g + 4, qb * 128:(qb + 1) * 128],
                                                              in0=bo.t[:, :].rearrange("p (h c) -> p h c", h=4),
                                                              in1=rc.t[:].rearrange("p (h c) -> p h c", h=4), op=ALU.mult),
                             reads=(bo.b, rc.b), writes=(attn_b[g],))

                scores(0); scores(1); pv(0); scores(2); pv(1); scores(3); pv(2); pv(3)
                P.stage_end(qT_b + [kT_b, Vt_b, PT[0].b, PT[1].b, rcp[0].b, rcp[1].b])

        def emit_rnn_B(l, i):
            t0 = i * T
            with ExitStack() as ss:
                xc = xcB; xc_b = xcB_b
                xcbf = SB("xcbf", [128, 8, T], BF16, stack=ss); xcbf_b = [P.buf(f"xcbf{k}", staged=True) for k in range(8)]
                X = dict(xr_b=[], xc=xc, xc_b=xc_b, xcbf=xcbf, xcbf_b=xcbf_b)
                Rs, Hs = rnn_tmps(ss, 4, 2)
                HB = [TB(SB(f"HB{j}", [128, T], stack=ss), P.buf(f"HB{j}", staged=True)) for j in range(4)]
                G = [TB(SB(f"G{j}", [128, T], stack=ss), P.buf(f"G{j}", staged=True)) for j in range(2)]
                for u in range(4):
                    s = P.wload(WIN[l][10 + u], 2048); wv = pairview(s)
                    for g in range(2):
                        kk = 2 * u + g
                        by = P.bank()
                        Gt = G[kk % 2]
                        P.mm(by, [(by.t[:, :], wv[:, k, g, :], hT[:, k, C0:C1], k == 0) for k in range(8)], reads=(hT_b, s.b))
                        P.op("act", lambda e: e.activation(out=Gt.t[:], in_=by.t[:, :], func=AF.Square), reads=(by.b,), writes=(Gt.b,))
                        P.op("pool", lambda e: e.tensor_scalar(out=Gt.t[:], in0=Gt.t[:], scalar1=0.044715, scalar2=1.0, op0=ALU.mult, op1=ALU.add),
                             reads=(Gt.b,), writes=(Gt.b,))
                        P.op("dve", lambda e: e.tensor_tensor(out=Gt.t[:], in0=Gt.t[:], in1=by.t[:, :], op=ALU.mult), reads=(Gt.b, by.b), writes=(Gt.b,))
                        P.op("act", lambda e: e.activation(out=Gt.t[:], in_=Gt.t[:], func=AF.Sigmoid, scale=GELU_C), reads=(Gt.b,), writes=(Gt.b,))
                        P.op("dve", lambda e: e.tensor_tensor(out=gy[:, kk, :], in0=Gt.t[:], in1=by.t[:, :], op=ALU.mult),
                             reads=(Gt.b, by.b), writes=(gy_b[kk],))
                gslot = P.wload(WG[l][0], 2048)
                for kk in range(8):
                    P.op("dve" if kk % 2 == 0 else "pool", lambda e: e.tensor_copy(out=xcbf[:, kk, :], in_=xc[:, kk, :]),
                         reads=(xc_b[kk],), writes=(xcbf_b[kk],))

                def post(kk, Hh):
                    hb_ = HB[kk % 4]
                    P.op("dve", lambda e: e.tensor_tensor(out=Hh.t[:], in0=Hh.t[:], in1=hb_.t[:], op=ALU.add), reads=(Hh.b, hb_.b), writes=(Hh.b,))
                    P.op("pool", lambda e: e.tensor_tensor(out=gy[:, kk, :], in0=Hh.t[:], in1=gy[:, kk, :], op=ALU.mult),
                         reads=(Hh.b, gy_b[kk]), writes=(gy_b[kk],))
                for bi in range(2):
                    kks = list(range(bi * 4, bi * 4 + 4))
                    for kk in kks:
                        hb_ = HB[kk % 4]
                        P.dma(hb_.t[:], hbst[kk, :, t0:t0 + T], hb_.b, reads=(hbreg[i][kk],), writes=(hb_.b,))
                    emit_chain_batch(l, 0, kks, X, gslot, Rs, lambda kk: Hs[kk % 2], post)
                    emit_merge_A(l, (2 * bi, 2 * bi + 1))
                allb = X["xr_b"] + X["xc_b"] + X["xcbf_b"] + [h.b for h in Hs] + [h.b for h in HB] + [G[0].b, G[1].b]
                for R in Rs:
                    allb += [R["r"].b, R["i"].b, R["s"].b]
                P.stage_end(allb)

        def emit_merge_A(l, us):
            for u in us:
                sga = P.wload(WIN[l][14 + u], 2048); sba = P.wload(WBR[l][0][u], 2048)
                for g in range(2):
                    m = 2 * u + g
                    A = mtmp[(m % 2) * 2]
                    b1 = P.bank(); b2 = P.bank()
                    P.mm(b1, [(b1.t[:, :], pairview(sga)[:, k, g, :], hT[:, k, C0:C1], k == 0) for k in range(8)], reads=(hT_b, sga.b))
                    P.mm(b2, [(b2.t[:, :], pairview(sba)[:, k, g, :], attnT[:, k, :], k == 0) for k in range(8)], reads=(attn_b[0], attn_b[1], sba.b))
                    P.op("act", lambda e: e.activation(out=A.t[:], in_=b1.t[:, :], func=AF.Sigmoid), reads=(b1.b,), writes=(A.b,))
                    P.op("dve", lambda e: e.tensor_tensor(out=mrg[:, m, :], in0=A.t[:], in1=b2.t[:, :], op=ALU.mult), reads=(A.b, b2.b), writes=(mrg_b[m],))

        def emit_merge(l):
            for u in range(4):
                sgr = P.wload(WIN[l][18 + u], 2048); sbr = P.wload(WBR[l][1][u], 2048)
                for g in range(2):
                    m = 2 * u + g
                    Bt = mtmp[(m % 2) * 2 + 1]
                    b3 = P.bank(); b4 = P.bank()
                    P.mm(b3, [(b3.t[:, :], pairview(sgr)[:, k, g, :], hT[:, k, C0:C1], k == 0) for k in range(8)], reads=(hT_b, sgr.b))
                    P.mm(b4, [(b4.t[:, :], pairview(sbr)[:, k, g, :], gy[:, k, :], k == 0) for k in range(8)], reads=tuple(gy_b) + (sbr.b,))
                    P.op("act", lambda e: e.activation(out=Bt.t[:], in_=b3.t[:, :], func=AF.Sigmoid), reads=(b3.b,), writes=(Bt.b,))
                    P.op("dve", lambda e: e.tensor_tensor(out=Bt.t[:], in0=Bt.t[:], in1=b4.t[:, :], op=ALU.mult), reads=(Bt.b, b4.b), writes=(Bt.b,))
                    P.op("pool", lambda e: e.tensor_tensor(out=mrg[:, m, :], in0=mrg[:, m, :], in1=Bt.t[:], op=ALU.add), reads=(mrg_b[m], Bt.b), writes=(mrg_b[m],))
            for u in range(4):
                s = P.wload(WOUT[l][u], 2048); wv = pairview(s)
                for g in range(2):
                    m = 2 * u + g
                    bo = P.bank()
                    P.mm(bo, [(bo.t[:, :], wv[:, k, g, :], mrg[:, k, :], k == 0) for k in range(8)], reads=tuple(mrg_b) + (s.b,))
                    P.op("dve", lambda e: e.tensor_tensor(out=xR[:, m, :], in0=xW[:, m, C0:C1], in1=bo.t[:, :], op=ALU.add),
                         reads=(xw[m], bo.b), writes=(xr_[m],))

        def build_esx(l):
            for h in range(8):
                P.op("dve", lambda e: e.tensor_scalar(out=esx[:, h // 4, h % 4, :], in0=zt[:], scalar1=est[:, l * 8 + h:l * 8 + h + 1],
                                                      scalar2=None, op0=ALU.add),
                     reads=(zt_b, es_b), writes=(esx_b,))

        def visit_0(sq, i):
            t0 = i * T
            with ExitStack() as ss:
                tm = SB("tm", [128, 4, D], stack=ss); tm_b = P.buf("tm", staged=True)
                src = bass.AP(xin.tensor, (sq * S + t0) * D, [[D, 128], [128 * D, 4], [1, D]])
                P.dma(tm[:, :, :], src, tm_b, writes=(tm_b,))
                for k in range(8):
                    bk = P.bank()
                    P.mm(bk, [("T", bk.t[:, b * 128:(b + 1) * 128], tm[:, b, k * 128:(k + 1) * 128], identf[:]) for b in range(4)],
                         reads=(tm_b, identf_b))
                    eng_ = "dve" if k % 2 == 0 else "act"
                    if eng_ == "dve":
                        P.op("dve", lambda e: e.tensor_copy(out=xR[:, k, :], in_=bk.t[:, :]), reads=(bk.b,), writes=(xr_[k],))
                    else:
                        P.op("act", lambda e: e.activation(out=xR[:, k, :], in_=bk.t[:, :], func=AF.Copy), reads=(bk.b,), writes=(xr_[k],))
                P.stage_end([tm_b])
            emit_ffn(0, 0)
            store_center(0, i)

        ystore_toks = {}

        def emit_final(sq, i):
            t0 = i * T
            with ExitStack() as ss:
                yT = SB("yT", [128, 8, T], stack=ss); yT_b = P.buf("yT", staged=True)
                tm2 = SB("tm2", [128, 4, D], stack=ss); tm2_b = P.buf("tm2", staged=True)
                emit_norm_final(yT, yT_b)
                for b in range(4):
                    for kq in range(2):
                        bk = P.bank()
                        P.mm(bk, [("T", bk.t[:, kk * 128:(kk + 1) * 128], yT[:, kq * 4 + kk, b * 128:(b + 1) * 128], identf[:]) for kk in range(4)],
                             reads=(yT_b, identf_b))
                        if kq == 0:
                            P.op("dve", lambda e: e.tensor_copy(out=tm2[:, b, kq * 512:(kq + 1) * 512], in_=bk.t[:, :]), reads=(bk.b,), writes=(tm2_b,))
                        else:
                            P.op("act", lambda e: e.activation(out=tm2[:, b, kq * 512:(kq + 1) * 512], in_=bk.t[:, :], func=AF.Copy),
                                 reads=(bk.b,), writes=(tm2_b,))
                dst = bass.AP(yout.tensor, (sq * S + t0) * D, [[D, 128], [128 * D, 4], [1, D]])
                ystore_toks["y"] = P.dma(dst, tm2[:, :, :], tm2_b, reads=(tm2_b,))
                P.stage_end([yT_b, tm2_b])

        def emit_norm_final(yT, yT_b):
            for k in range(8):
                P.op("act", lambda e: e.activation(out=hT[:, k, C0:C1], in_=xR[:, k, :], func=AF.Square), reads=(xr_[k],), writes=(hT_b,))
            bk = P.bank()
            P.mm(bk, [(bk.t[:, :], onesb[:], hT[:, k, C0:C1], k == 0) for k in range(8)], reads=(hT_b, onesb_b))
            P.op("act", lambda e: e.activation(out=rstd[:, C0:C1], in_=bk.t[:, :], func=AF.Sqrt, scale=1.0 / D, bias=EPS), reads=(bk.b,), writes=(rstd_b,))
            P.op("dve", lambda e: e.reciprocal(out=rstd[:, C0:C1], in_=rstd[:, C0:C1]), reads=(rstd_b,), writes=(rstd_b,))
            for k in range(8):
                P.op("dve", lambda e: e.scalar_tensor_tensor(out=yT[:, k, :], in0=xR[:, k, :], scalar=prm[:, 0, P_FIN + k:P_FIN + k + 1],
                                                             in1=rstd[:, C0:C1], op0=ALU.mult, op1=ALU.mult),
                     reads=(xr_[k], rstd_b, prm_b), writes=(yT_b,))

        def visit_B(sq, l, i):
            pp = l % 2
            if i == 0:
                load_window(pp, i)
            P.dma(xcB[:, :, :], bass.AP(xcst.tensor, i * T, [[S, 128], [128 * S, 8], [1, T]]), xcl_b, reads=(xcreg[i],), writes=xcB_b)
            emit_norm(0, WIN_W, lambda k: prm[:, l, P_MIX + k:P_MIX + k + 1], hT, hT_b)
            emit_attention(l, i)
            emit_rnn_B(l, i)
            emit_merge(l)
            hook = (lambda: load_window(pp, i + 1)) if i + 1 < NT else None
            emit_ffn(l, 1, hook)
            if l + 1 < DEPTH:
                emit_ffn(l + 1, 0)
                store_center((l + 1) % 2, i)
            else:
                emit_final(sq, i)

        for sq in range(NSEQ):
            for i in range(NT):
                visit_0(sq, i)
            for l in range(DEPTH):
                for d in range(2):
                    for k in range(8):
                        P.op("pool", lambda e: e.memset(carry[:, d, k:k + 1], 0.0), writes=(carry_b[d][k],))
                build_esx(l)
                for i in range(NT - 1, -1, -1):
                    visit_A(l, i, i == NT - 1, (i - 1) if i > 0 else None)
                for i in range(NT):
                    visit_B(sq, l, i)
        P._wait("sp", list(ystore_toks.values()))
        build.ninstr = P.ninstr
    return nc


_CACHE = {}


def _get_nc(cfg_key):
    if cfg_key not in _CACHE:
        _CACHE[cfg_key] = build(Cfg(*cfg_key))
    return _CACHE[cfg_key]


def run_cores(seqs_per_core, weights, S, DEPTH):
    NSEQ = seqs_per_core[0].shape[0]
    nc = _get_nc((S, NSEQ, DEPTH))
    ident = np.eye(128, dtype=np.float32)
    oh = make_onehot()
    in_maps = []
    for c in range(len(seqs_per_core)):
        m = {n: np.ascontiguousarray(weights[n], dtype=np.float32) for n in WEIGHT_NAMES}
        m["xin"] = np.ascontiguousarray(seqs_per_core[c], dtype=np.float32)
        m["ident"] = ident
        m["onehot"] = oh
        in_maps.append(m)
    res = run_bass_kernel_spmd(nc, in_maps, core_ids=list(range(len(seqs_per_core))))
    return [r["y"] for r in res.results]


def kernel(**inputs):
    xp = np.asarray(inputs["x_prompt"], dtype=np.float32)
    xs = np.asarray(inputs["x_sample"], dtype=np.float32)
    seqs = [xp[0], xp[1]] + [xs[i] for i in range(8)]
    per_core = []
    for c in range(8):
        second = seqs[8 + c] if c < 2 else seqs[c]
        per_core.append(np.stack([seqs[c], second], axis=0))
    weights = {n: inputs[n] for n in WEIGHT_NAMES}
    ys = run_cores(per_core, weights, 8192, 4)
    out = [ys[c][0] for c in range(8)] + [ys[0][1], ys[1][1]]
    y_prompt = np.stack(out[0:2], axis=0).astype(np.float32)
    y_sample = np.stack(out[2:10], axis=0).astype(np.float32)
    return (y_prompt, y_sample)
```
